# Optimizing a Trainium2 kernel written in Bass

```python
import jax, jax.numpy as jnp
from jax import lax
import numpy as np

D_MODEL = 1024
BATCH = 4
SEQ = 4096
DEPTH = 4
DEC_BATCH = 32
DEC_SEQ = 16
PAST_LEN = 1024

CHUNK = 64
SGU_CHUNK = 128
SGU_WIDTH = D_MODEL // 2
SGU_GROUPS = 4
SGU_GROUP_DIM = SGU_WIDTH // SGU_GROUPS
SB_HEADS = 8
SB_HEAD_DIM = 64
SB_WIDTH = SB_HEADS * SB_HEAD_DIM
MIX_WIDTH = SGU_WIDTH + SB_WIDTH
IN_WIDTH = 2 * SGU_WIDTH + 3 * SB_WIDTH
SB_Q_BLOCK = 128
N_MEM = 256
MEM_HEADS = 4
MEM_HEAD_DIM = D_MODEL // MEM_HEADS
D_FF = -(-8 * D_MODEL // (3 * 256)) * 256
EPS = 1e-6

kernel_name = 'hybrid_sgu_stickbreak_stream_step'


def _rmsnorm(x, g):
    xf = x.astype(jnp.float32)
    y = xf * lax.rsqrt(jnp.mean(xf * xf, axis=-1, keepdims=True) + EPS)
    return (y * g.astype(jnp.float32)).astype(x.dtype)


def _mixer_inputs(h, w_in, g_v):
    B, T = h.shape[0], h.shape[1]
    proj = h @ w_in
    u, va, q, k, vb = jnp.split(
        proj, [SGU_WIDTH, 2 * SGU_WIDTH, 2 * SGU_WIDTH + SB_WIDTH, 2 * SGU_WIDTH + 2 * SB_WIDTH], axis=-1)
    u = jax.nn.gelu(u)
    va = _rmsnorm(jax.nn.gelu(va).reshape(B, T, SGU_GROUPS, SGU_GROUP_DIM), g_v)
    q = q.reshape(B, T, SB_HEADS, SB_HEAD_DIM)
    k = k.reshape(B, T, SB_HEADS, SB_HEAD_DIM)
    vb = vb.reshape(B, T, SB_HEADS, SB_HEAD_DIM)
    return u, va, q, k, vb


def _sgu_mix(v, w_s, b_s):
    L = v.shape[2]
    w = jnp.tril(w_s[:, :L, :L])
    bias = jnp.transpose(b_s[:, :L])[:, :, None]
    return jnp.einsum('gts,bcsge->bctge', w, v) + bias


def _stick_breaking(q, k, v, q_pos, k_pos):
    z = jnp.einsum('bqhd,bkhd->bhqk', q.astype(jnp.float32), k.astype(jnp.float32)) * (SB_HEAD_DIM ** -0.5)
    mask = k_pos[None, :] < q_pos[:, None]
    log_keep = jnp.where(mask, jax.nn.log_sigmoid(-z), 0.0)
    between = lax.cumsum(log_keep, axis=3, reverse=True) - log_keep
    a = jnp.where(mask, jnp.exp(jax.nn.log_sigmoid(z) + between), 0.0)
    o = jnp.einsum('bhqk,bkhd->bqhd', a, v.astype(jnp.float32))
    return o.astype(q.dtype)


def _sb_prompt(q, k, v):
    T = q.shape[1]
    pos = jnp.arange(T, dtype=jnp.int32)

    def blk(i):
        start = i * SB_Q_BLOCK
        qb = lax.dynamic_slice_in_dim(q, start, SB_Q_BLOCK, axis=1)
        pb = lax.dynamic_slice_in_dim(pos, start, SB_Q_BLOCK)
        return _stick_breaking(qb, k, v, pb, pos)

    out = lax.map(blk, jnp.arange(T // SB_Q_BLOCK, dtype=jnp.int32))
    return jnp.moveaxis(out, 0, 1).reshape(q.shape)


def _merge(a_out, sb_out, g_a, g_b, w_out):
    B, T = a_out.shape[0], a_out.shape[1]
    cat = jnp.concatenate([_rmsnorm(a_out, g_a), _rmsnorm(sb_out.reshape(B, T, SB_WIDTH), g_b)], axis=-1)
    return cat @ w_out


def _mem_kv(mem, w_mk, w_mv):
    B = mem.shape[0]
    mk = (mem @ w_mk).reshape(B, N_MEM, MEM_HEADS, MEM_HEAD_DIM)
    mv = (mem @ w_mv).reshape(B, N_MEM, MEM_HEADS, MEM_HEAD_DIM)
    return mk, mv


def _mem_attend(h, mk, mv, w_mq, w_mo):
    B, T = h.shape[0], h.shape[1]
    q = (h @ w_mq).reshape(B, T, MEM_HEADS, MEM_HEAD_DIM)
    s = jnp.einsum('bqhd,bkhd->bhqk', q.astype(jnp.float32), mk.astype(jnp.float32)) * (MEM_HEAD_DIM ** -0.5)
    p = jax.nn.softmax(s, axis=-1)
    o = jnp.einsum('bhqk,bkhd->bqhd', p, mv.astype(jnp.float32)).astype(h.dtype)
    return o.reshape(B, T, D_MODEL) @ w_mo


def _swiglu(h, w_gate, w_up, w_down):
    return (jax.nn.silu(h @ w_gate) * (h @ w_up)) @ w_down


def setup_inputs(seed: int = 0) -> dict:
    key = jax.random.key(seed)
    ks = jax.random.split(key, 32)
    f32 = jnp.float32

    def nrm(k, shape, scale=1.0):
        return jax.random.normal(k, shape, f32) * scale

    def gain(k, shape):
        return 1.0 + 0.05 * jax.random.normal(k, shape, f32)

    L = DEPTH
    return {
        'x_prompt': nrm(ks[0], (BATCH, SEQ, D_MODEL)),
        'x_sample': nrm(ks[1], (DEC_BATCH, DEC_SEQ, D_MODEL)),
        'cache_sb_k': nrm(ks[2], (L, DEC_BATCH, PAST_LEN, SB_HEADS, SB_HEAD_DIM)),
        'cache_sb_v': nrm(ks[3], (L, DEC_BATCH, PAST_LEN, SB_HEADS, SB_HEAD_DIM)),
        'cache_mem_k': nrm(ks[4], (L, DEC_BATCH, N_MEM, MEM_HEADS, MEM_HEAD_DIM)),
        'cache_mem_v': nrm(ks[5], (L, DEC_BATCH, N_MEM, MEM_HEADS, MEM_HEAD_DIM)),
        'mem_prompt': nrm(ks[6], (BATCH, N_MEM, D_MODEL)),
        'ln_mix': gain(ks[7], (L, D_MODEL)),
        'w_in': nrm(ks[8], (L, D_MODEL, IN_WIDTH), D_MODEL ** -0.5),
        'g_sgu_v': gain(ks[9], (L, SGU_GROUPS, SGU_GROUP_DIM)),
        'w_sgu': nrm(ks[10], (L, SGU_GROUPS, SGU_CHUNK, SGU_CHUNK), SGU_CHUNK ** -0.5),
        'b_sgu': 1.0 + 0.01 * jax.random.normal(ks[11], (L, SGU_GROUPS, SGU_CHUNK), f32),
        'g_out_sgu': gain(ks[12], (L, SGU_WIDTH)),
        'g_out_sb': gain(ks[13], (L, SB_WIDTH)),
        'w_out': nrm(ks[14], (L, MIX_WIDTH, D_MODEL), MIX_WIDTH ** -0.5),
        'ln_mem': gain(ks[15], (L, D_MODEL)),
        'w_mq': nrm(ks[16], (L, D_MODEL, D_MODEL), D_MODEL ** -0.5),
        'w_mk': nrm(ks[17], (L, D_MODEL, D_MODEL), D_MODEL ** -0.5),
        'w_mv': nrm(ks[18], (L, D_MODEL, D_MODEL), D_MODEL ** -0.5),
        'w_mo': nrm(ks[19], (L, D_MODEL, D_MODEL), D_MODEL ** -0.5),
        'ln_ffn': gain(ks[20], (L, D_MODEL)),
        'w_ffn_gate': nrm(ks[21], (L, D_MODEL, D_FF), D_MODEL ** -0.5),
        'w_ffn_up': nrm(ks[22], (L, D_MODEL, D_FF), D_MODEL ** -0.5),
        'w_ffn_down': nrm(ks[23], (L, D_FF, D_MODEL), D_FF ** -0.5),
        'ln_final': gain(ks[24], (D_MODEL,)),
    }


def reference(x_prompt, x_sample, cache_sb_k, cache_sb_v, cache_mem_k, cache_mem_v, mem_prompt,
              ln_mix, w_in, g_sgu_v, w_sgu, b_sgu, g_out_sgu, g_out_sb, w_out,
              ln_mem, w_mq, w_mk, w_mv, w_mo, ln_ffn, w_ffn_gate, w_ffn_up, w_ffn_down, ln_final):
    x = x_prompt
    B, T = x.shape[0], x.shape[1]
    sbk_p, sbv_p, mk_p, mv_p = [], [], [], []
    for l in range(DEPTH):
        h = _rmsnorm(x, ln_mix[l])
        u, va, q, k, vb = _mixer_inputs(h, w_in[l], g_sgu_v[l])
        va_c = va.reshape(B, T // SGU_CHUNK, SGU_CHUNK, SGU_GROUPS, SGU_GROUP_DIM)
        a_out = u * _sgu_mix(va_c, w_sgu[l], b_sgu[l]).reshape(B, T, SGU_WIDTH)
        sb_out = _sb_prompt(q, k, vb)
        x = x + _merge(a_out, sb_out, g_out_sgu[l], g_out_sb[l], w_out[l])
        mk, mv = _mem_kv(mem_prompt, w_mk[l], w_mv[l])
        x = x + _mem_attend(_rmsnorm(x, ln_mem[l]), mk, mv, w_mq[l], w_mo[l])
        x = x + _swiglu(_rmsnorm(x, ln_ffn[l]), w_ffn_gate[l], w_ffn_up[l], w_ffn_down[l])
        sbk_p.append(k)
        sbv_p.append(vb)
        mk_p.append(mk)
        mv_p.append(mv)
    y_prompt = _rmsnorm(x, ln_final)

    x = x_sample
    Bs, n = x.shape[0], x.shape[1]
    P = cache_sb_k.shape[2]
    k_pos = jnp.arange(P + n, dtype=jnp.int32)
    q_pos = P + jnp.arange(n, dtype=jnp.int32)
    sbk_s, sbv_s, sguv_s = [], [], []
    for l in range(DEPTH):
        h = _rmsnorm(x, ln_mix[l])
        u, va, q, k, vb = _mixer_inputs(h, w_in[l], g_sgu_v[l])
        a_out = u * _sgu_mix(va[:, None], w_sgu[l], b_sgu[l]).reshape(Bs, n, SGU_WIDTH)
        k_all = jnp.concatenate([cache_sb_k[l], k], axis=1)
        v_all = jnp.concatenate([cache_sb_v[l], vb], axis=1)
        sb_out = _stick_breaking(q, k_all, v_all, q_pos, k_pos)
        x = x + _merge(a_out, sb_out, g_out_sgu[l], g_out_sb[l], w_out[l])
        x = x + _mem_attend(_rmsnorm(x, ln_mem[l]), cache_mem_k[l], cache_mem_v[l], w_mq[l], w_mo[l])
        x = x + _swiglu(_rmsnorm(x, ln_ffn[l]), w_ffn_gate[l], w_ffn_up[l], w_ffn_down[l])
        sbk_s.append(k)
        sbv_s.append(vb)
        sguv_s.append(va)
    y_sample = _rmsnorm(x, ln_final)

    sb_k_prompt = jnp.stack(sbk_p)
    sb_v_prompt = jnp.stack(sbv_p)
    mem_k_prompt = jnp.stack(mk_p)
    mem_v_prompt = jnp.stack(mv_p)
    sb_k_sample = jnp.stack(sbk_s)
    sb_v_sample = jnp.stack(sbv_s)
    sgu_v_sample = jnp.stack(sguv_s)
    return (y_prompt, y_sample, sb_k_prompt, sb_v_prompt, mem_k_prompt, mem_v_prompt, sb_k_sample, sb_v_sample, sgu_v_sample)
```

```python
import contextlib
import os
import numpy as np
import concourse.bass as bass
import concourse.mybir as mybir
from concourse.bass_utils import run_bass_kernel_spmd

F32 = mybir.dt.float32
BF16 = mybir.dt.bfloat16
AF = mybir.ActivationFunctionType
ALU = mybir.AluOpType
AX = mybir.AxisListType

D = 1024
SEQ = 4096
DEPTH = 4
NSTREAM = 4
DSEQ = 16
PAST = 1024
NMEM = 256
DFF = 2816
NFC = DFF // 128
EPS = 1e-6
TILE = 512

ENGS = ("pe", "act", "dve", "pool", "sp")


def I(meth, *a, **k):
    f = lambda e: getattr(e, meth)(*a, **k)
    f.desc = (meth, a, k)
    return f


PSUM_BUFS = {"mm0", "mm1", "zp0", "zp1", "bp0", "bp1", "op_", "trp"}


class Buf:
    __slots__ = ("name", "lw", "rd", "dma_tot")

    def __init__(self, name):
        self.name = name
        self.lw = None
        self.rd = {}
        self.dma_tot = 0


class Prog:
    CH = 12000

    def __init__(self):
        self.q = {e: [] for e in ENGS}
        self.n = {e: 0 for e in ENGS}
        self.waited = {e: {} for e in ENGS}
        self.semkeys = {}
        self.out_events = {}
        self.nrec = 0
        self.kstop = int(os.environ.get("KSTOP", "1000000000"))

    def _wait_list(self, eng, deps, pe_sync=False):
        waits = []
        for (key, val, src) in deps:
            if src == eng and eng == "pe" and not pe_sync:
                continue
            if self.waited[eng].get(key, 0) >= val:
                continue
            self.waited[eng][key] = val
            waits.append((key, val))
        return waits

    def op(self, eng, fn, reads=(), writes=(), pe_sync=False):
        self.nrec += 1
        if self.nrec > self.kstop:
            return
        deps = []
        for b in reads:
            if b.lw is not None:
                deps.append(b.lw)
            if b.name in PSUM_BUFS:
                deps.extend(ev for ev in b.rd.values() if ev[2] != eng)
        for b in writes:
            if b.lw is not None:
                deps.append(b.lw)
            deps.extend(b.rd.values())
        i = self.n[eng]
        self.n[eng] += 1
        key = ("e", eng, i // self.CH)
        ev = (key, i % self.CH + 1, eng)
        self.semkeys[key] = True
        waits = self._wait_list(eng, deps, pe_sync)
        self.q[eng].append((waits, fn, (key, 1)))
        for b in writes:
            b.lw = ev
            b.rd = {}
        for b in reads:
            if b not in writes:
                b.rd[key] = ev

    def dma(self, eng, fn, src, dst, side, is_output=False):
        self.nrec += 1
        if self.nrec > self.kstop:
            return
        deps = []
        if src.lw is not None:
            deps.append(src.lw)
        if dst.lw is not None:
            deps.append(dst.lw)
        deps.extend(dst.rd.values())
        key = ("d", side.name, eng)
        self.semkeys[key] = True
        if not isinstance(side.dma_tot, dict):
            side.dma_tot = {}
        tot = side.dma_tot.get(key, 0)
        if tot > 0:
            deps.append((key, tot, "dma"))
        tot += 16
        side.dma_tot[key] = tot
        ev = (key, tot, "dma")
        waits = self._wait_list(eng, deps)
        self.q[eng].append((waits, fn, (key, 16)))
        dst.lw = ev
        dst.rd = {}
        src.rd[key] = ev
        if is_output:
            self.out_events[key] = ev

    def finish(self):
        waits = self._wait_list("sp", list(self.out_events.values()))
        self.q["sp"].append((waits, None, None))


def build(n_tiles=SEQ // TILE, depth=DEPTH, with_sample=True):
    nc = bass.Bass("TRN2", target_bir_lowering=False)
    P = Prog()
    L = DEPTH

    def din(name, shape, dt=F32):
        return nc.dram_tensor(name, list(shape), dt, kind="ExternalInput").ap()

    def dout(name, shape, dt=F32):
        return nc.dram_tensor(name, list(shape), dt, kind="ExternalOutput").ap()

    xp = din("xp", [SEQ, D])
    xs = din("xs", [64, D])
    csk = din("csk", [L, NSTREAM, PAST, 512])
    csv = din("csv", [L, NSTREAM, PAST, 512])
    cmk = din("cmk", [L, NSTREAM, NMEM, D])
    cmv = din("cmv", [L, NSTREAM, NMEM, D])
    memp = din("memp", [NMEM, D])
    w_in = din("w_in", [L, D, 2560])
    w_out = din("w_out", [L, D, D])
    w_mq = din("w_mq", [L, D, D])
    w_mk = din("w_mk", [L, D, D])
    w_mv = din("w_mv", [L, D, D])
    w_mo = din("w_mo", [L, D, D])
    w_g = din("w_g", [L, D, DFF])
    w_u = din("w_u", [L, D, DFF])
    w_d = din("w_d", [L, DFF, D])
    gains_fm = din("gains_fm", [128, 3 * L * 8 + 8 + 2 * L * 4])
    gv_rep = din("gv_rep", [128, L * 512])
    bs_rep = din("bs_rep", [128, L * 512])
    bs_rep_s = din("bs_rep_s", [128, L * 256])
    wsT = din("wsT", [L, 4, 128, 128])
    wsT_s = din("wsT_s", [L, 4, 64, 64])
    masks = din("masks", [128, 960])
    lnf_rep = din("lnf_rep", [128, D])

    y_p = dout("y_p", [SEQ, D])
    y_s = dout("y_s", [64, D])
    sbk_p = dout("sbk_p", [L, SEQ, 512])
    sbv_p = dout("sbv_p", [L, SEQ, 512])
    mk_p = dout("mk_p", [L, NMEM, D])
    mv_p = dout("mv_p", [L, NMEM, D])
    sbk_s = dout("sbk_s", [L, 64, 512])
    sbv_s = dout("sbv_s", [L, 64, 512])
    sguv_s = dout("sguv_s", [L, 64, 512])

    ktc_d = nc.dram_tensor("ktc_d", [L, 4, 128, SEQ], BF16).ap()
    vc_d = nc.dram_tensor("vc_d", [L, SEQ, 512], BF16).ap()
    mkT_d = nc.dram_tensor("mkT_d", [L, 128, 8, NMEM], BF16).ap()
    mvb_d = nc.dram_tensor("mvb_d", [L, 128, 2, D], BF16).ap()

    es = contextlib.ExitStack()

    def sb(name, shape, dt):
        return es.enter_context(nc.sbuf_tensor(name, list(shape), dt))

    def ps(name, shape, dt=F32):
        return es.enter_context(nc.psum_tensor(name, list(shape), dt))

    x_t = sb("x_t", [128, 4, D], F32)
    hT = sb("hT", [128, 8, TILE], BF16)
    xn = sb("xn", [128, D], BF16)
    sqs = sb("sqs", [128, D], F32)
    NW = 3
    wr = [sb(f"wr{i}", [128, 8, 512], BF16) for i in range(NW)]
    qk = sb("qk", [128, 8, TILE], BF16)
    vn_b = sb("vn_b", [128, 4, 512], BF16)
    stg = [sb(f"stg{i}", [128, 512], F32) for i in range(3)]
    ktc = [sb(f"ktc{i}", [128, SEQ], BF16) for i in range(2)]
    vcs = [sb(f"vcs{i}", [128, (SEQ - TILE) // 128, 128], BF16) for i in range(2)]
    e_t = [sb(f"e_t{i}", [128, 512], F32) for i in range(2)]
    lk_t = [sb(f"lk_t{i}", [128, 512], BF16) for i in range(2)]
    p_t = [sb(f"p_t{i}", [128, 512], BF16) for i in range(2)]
    a_t = [sb(f"a_t{i}", [128, 512], BF16) for i in range(2)]
    s_t = sb("s_t", [128, 512], BF16)
    uT = sb("uT", [128, 4, TILE], BF16)
    vg = sb("vg", [128, 512], F32)
    vsq = sb("vsq", [128, 512], F32)
    vnA = sb("vnA", [128, 4, 512], BF16)
    aT = sb("aT", [128, 4, TILE], BF16)
    sbTb = sb("sbTb", [128, 4, TILE], BF16)
    sq4 = sb("sq4", [128, 4, TILE], BF16)
    actT = sb("actT", [128, 8, TILE], BF16)
    mkT = sb("mkT", [128, 8, NMEM], BF16)
    mvb = sb("mvb", [128, 2, D], BF16)
    pT_t = [sb(f"pT_t{i}", [128, 2, TILE], BF16) for i in range(2)]
    rden = sb("rden", [128, TILE], F32)
    small = sb("small", [128, 64], F32)
    gfm = sb("gfm", [128, 3 * L * 8 + 8 + 2 * L * 4], F32)
    gvr = sb("gvr", [128, 512], F32)
    bsr = sb("bsr", [128, 512], F32)
    bsr_s = sb("bsr_s", [128, 256], F32)
    wsT_sb = sb("wsT_sb", [128, L * 4, 128], BF16)
    wsTs_sb = sb("wsTs_sb", [64, L * 4, 64], BF16)
    msk_f = sb("msk_f", [128, 960], F32)
    tri_m = sb("tri_m", [128, 128], F32)
    negtri = sb("negtri", [128, 128], BF16)
    negones = sb("negones", [128, 128], BF16)
    ones_b = sb("ones_b", [128, 128], BF16)
    ident = sb("ident", [128, 128], BF16)
    bd_m = sb("bd_m", [64, 64], BF16)
    kc_tok = [ktc[i][:, 0:2048].rearrange("p (s n) -> p s n", s=4) for i in range(2)]
    kcT = [ktc[i][:, 2048:4096].rearrange("p (s a k) -> p s a k", s=4, a=4) for i in range(2)]
    vc_tok = [vcs[i][:, 0:16, :].rearrange("p (s a) k -> p s (a k)", s=4) for i in range(2)]
    mk_tok = sb("mk_tok", [128, 2, D], BF16)
    memT = actT[:, 0:4, :].rearrange("p a (b k) -> p (a b) k", b=2)
    smk1 = sb("smk1", [128, 8, NMEM], BF16)
    smv1 = sb("smv1", [128, 2, D], BF16)

    mm = [ps(f"mm{i}", [128, 512]) for i in range(2)]
    zp = [ps(f"zp{i}", [128, 512]) for i in range(2)]
    bp = [ps(f"bp{i}", [128, 512]) for i in range(2)]
    op_ = ps("op_", [128, 512])
    trp = ps("trp", [128, 1024], BF16)

    B = {}

    ALIAS = {"kc_tok0": "ktc0", "kc_tok1": "ktc1", "kcT0": "ktc0", "kcT1": "ktc1", "vc_tok0": "vcs0", "vc_tok1": "vcs1",
             "memT": "actT"}

    def b(name):
        name = ALIAS.get(name, name)
        if name not in B:
            B[name] = Buf(name)
        return B[name]

    GOFF_LNMIX, GOFF_LNMEM, GOFF_LNFFN = 0, L * 8, 2 * L * 8
    GOFF_LNF = 3 * L * 8
    GOFF_GA = GOFF_LNF + 8
    GOFF_GB = GOFF_GA + L * 4

    def ld(eng, dst_ap, src_ap, dname, sname="dram_in"):
        P.dma(eng, I("dma_start", out=dst_ap, in_=src_ap), b(sname), b(dname), b(dname))

    ld("sp", gfm[:], gains_fm[:, :], "gfm")
    ld("sp", msk_f[:], masks[:, :], "msk_f")
    ld("pool", wsT_sb[:], wsT.rearrange("l g s t -> s (l g) t"), "wsT_sb")
    ld("pool", wsTs_sb[:], wsT_s.rearrange("l g s t -> s (l g) t"), "wsTs_sb")
    P.op("dve", I("tensor_copy", out=tri_m[:], in_=msk_f[:, 0:128]), [b("msk_f")], [b("tri_m")])
    P.op("dve", I("tensor_copy", out=negtri[:], in_=msk_f[:, 128:256]), [b("msk_f")], [b("negtri")])
    P.op("dve", I("tensor_copy", out=bd_m[:], in_=msk_f[0:64, 256:320]), [b("msk_f")], [b("bd_m")])
    epsb = sb("epsb", [128, 1], F32)
    P.op("pool", I("memset", epsb[:], EPS), [], [b("epsb")])
    P.op("pool", I("memset", negones[:], -1.0), [], [b("negones")])
    P.op("pool", I("memset", ones_b[:], 1.0), [], [b("ones_b")])
    P.op("dve", I("tensor_copy", out=ident[:], in_=msk_f[:, 832:960]), [b("msk_f")], [b("ident")])
    incl = sb("incl", [128, 128], BF16)
    P.op("dve", I("tensor_tensor", out=incl[:], in0=tri_m[:], in1=ident[:], op=ALU.add),
         [b("tri_m"), b("ident")], [b("incl")])
    for i in range(L * 4):
        P.op("pool", I("tensor_tensor", out=wsT_sb[:, i, :], in0=wsT_sb[:, i, :], in1=incl[:], op=ALU.mult),
             [b("incl"), b("wsT_sb")], [b("wsT_sb")])
        P.op("pool", I("tensor_tensor", out=wsTs_sb[:, i, :], in0=wsTs_sb[:, i, :], in1=bd_m[:], op=ALU.mult),
             [b("bd_m"), b("wsTs_sb")], [b("wsTs_sb")])

    wstate = {"i": 0}

    def wload(w_ap, l, r0, nkc, c0, ncols):
        s = wstate["i"] % NW
        wstate["i"] += 1
        t = wr[s]
        src = w_ap[l, r0:r0 + nkc * 128, c0:c0 + ncols].rearrange("(kc p) n -> p kc n", p=128)
        P.dma("pool", I("dma_start", out=t[:, 0:nkc, 0:ncols], in_=src), b("dram_in"), b(f"wr{s}"), b(f"wr{s}"))
        return t, b(f"wr{s}")

    mmstate = {"i": 0}

    def mmbank():
        i = mmstate["i"] % 2
        mmstate["i"] += 1
        return mm[i], b(f"mm{i}")

    stgstate = {"i": 0}

    def stage():
        i = stgstate["i"] % 3
        stgstate["i"] += 1
        return stg[i], b(f"stg{i}")

    trstate = {"i": 0}

    mm1_bf = mm[1][:].bitcast(BF16)

    def trbank():
        i = trstate["i"] % 2
        trstate["i"] += 1
        return trp[:, i * 512:(i + 1) * 512], b("trp")

    def store(dst_ap, src_ap, sbuf_buf, out_name):
        P.dma("sp", I("dma_start", out=dst_ap, in_=src_ap), sbuf_buf, b(out_name), sbuf_buf, is_output=True)

    def rms_to_hT(goff, l, NS, TP):
        NT = NS * TP
        for j in range(NS):
            bx = b(f"x{j}")
            P.op("act", I("activation", out=sqs[:TP, :], in_=x_t[:TP, j, :], func=AF.Square),
                 [bx], [b("sqs")])
            P.op("dve", I("tensor_reduce", out=small[:TP, j:j + 1], in_=sqs[:TP, :], axis=AX.X, op=ALU.add),
                 [b("sqs")], [b("small")])
            pow_rstd(small[:TP, j:j + 1], D, TP)
            P.op("dve", I("tensor_scalar", out=xn[:TP, :], in0=x_t[:TP, j, :], scalar1=small[:TP, j:j + 1],
                                                       scalar2=None, op0=ALU.mult), [bx, b("small")], [b("xn")])
            for half in range(2):
                tp_ap, tb = trbank()
                for cc in range(4):
                    c = half * 4 + cc
                    P.op("pe", I("transpose",
                        out=tp_ap[:, cc * 128:cc * 128 + TP], in_=xn[:TP, c * 128:(c + 1) * 128], identity=ident[:TP, :TP]),
                        [b("xn"), b("ident")], [tb])
                for cc in range(4):
                    c = half * 4 + cc
                    P.op("act", I("activation",
                        out=hT[:, c, j * TP:(j + 1) * TP], in_=tp_ap[:, cc * 128:cc * 128 + TP], func=AF.Identity,
                        scale=gfm[:, goff + l * 8 + c:goff + l * 8 + c + 1]), [tb, b("gfm")], [b("hT")])

    def fm_proj(wt, wb, nkc, fcs, rhs_fn, rhs_bufs, NT, evac):
        for (fc, col0) in fcs:
            pt, pb = mmbank()
            for kc in range(nkc):
                P.op("pe", I("matmul",
                    pt[:, :NT], lhsT=wt[:, kc, col0:col0 + 128], rhs=rhs_fn(kc), start=(kc == 0), stop=(kc == nkc - 1)),
                    [wb] + rhs_bufs, [pb])
            evac(fc, pt, pb)

    def tm_proj(wt, wb, nkc, lhs_fn, lhs_bufs, ncols, NS, TP, evac, kc0=0):
        for j in range(NS):
            pt, pb = mmbank()
            for kc in range(nkc):
                P.op("pe", I("matmul",
                    pt[:TP, :ncols], lhsT=lhs_fn(kc, j), rhs=wt[:, kc0 + kc, 0:ncols], start=(kc == 0), stop=(kc == nkc - 1)),
                    [wb] + lhs_bufs, [pb])
            evac(j, pt, pb)

    def pow_rstd(ap, n, TP):
        P.op("act", I("activation", out=ap, in_=ap, func=AF.Ln, scale=1.0 / n, bias=epsb[:TP, 0:1]), [b("small"), b("epsb")], [b("small")])
        P.op("act", I("activation", out=ap, in_=ap, func=AF.Exp, scale=-0.5), [b("small")], [b("small")])

    def sb_block(ebuf_i, zt, zb, c0, kparts, first, mask_fn, av_fn):
        i = ebuf_i
        e_, lk_, p_, a_ = e_t[i], lk_t[i], p_t[i], a_t[i]
        be, blk, bpp, ba = b(f"e{i}"), b(f"lk{i}"), b(f"p{i}"), b(f"a{i}")
        bt, bb = bp[i], b(f"bp{i}")
        kp = kparts
        P.op("act", I("activation", out=e_[:kp, c0:], in_=zt[:kp, c0:], func=AF.Exp), [zb], [be])
        if mask_fn is not None:
            mask_fn(e_, be)
        P.op("act", I("activation", out=lk_[:kp, c0:], in_=e_[:kp, c0:], func=AF.Ln, bias=1.0, scale=1.0), [be], [blk])
        P.op("pe", I("matmul", bt[:kp, c0:], lhsT=negtri[:kp, :kp], rhs=lk_[:kp, c0:], start=True, stop=first),
             [b("negtri"), blk], [bb])
        if not first:
            P.op("pe", I("matmul", bt[:kp, c0:], lhsT=negones[:, :kp], rhs=s_t[:, c0:], start=False, stop=True),
                 [b("negones"), b("s_t")], [bb])
        P.op("act", I("activation", out=p_[:kp, c0:], in_=bt[:kp, c0:], func=AF.Exp), [bb], [bpp])
        P.op("dve", I("tensor_tensor", out=a_[:kp, c0:], in0=e_[:kp, c0:], in1=p_[:kp, c0:], op=ALU.mult),
             [be, bpp], [ba])
        P.op("dve", I("tensor_tensor", out=s_t[:kp, c0:], in0=s_t[:kp, c0:], in1=lk_[:kp, c0:], op=ALU.add),
             [blk, b("s_t")], [b("s_t")])
        av_fn(a_, ba)

    def layer(l, t, sample):
        NS, TP = (1, 64) if sample else (4, 128)
        NT = NS * TP
        tok0 = t * TILE
        bxs = [b(f"x{j}") for j in range(NS)]

        ld("sp", gvr[:], gv_rep[:, l * 512:(l + 1) * 512], "gvr")
        if sample:
            ld("sp", bsr_s[:], bs_rep_s[:, l * 256:(l + 1) * 256], "bsr_s")
        else:
            ld("sp", bsr[:], bs_rep[:, l * 512:(l + 1) * 512], "bsr")
        rms_to_hT(GOFF_LNMIX, l, NS, TP)
        hb = [b("hT")]
        wt, wb = wload(w_in, l, 0, 8, 0, 512)
        fm_proj(wt, wb, 8, [(g, g * 128) for g in range(4)], lambda kc: hT[:, kc, :NT], hb, NT,
                lambda fc, pt, pb: P.op("act", I("activation", out=uT[:, fc, :NT], in_=pt[:, :NT], func=AF.Gelu_apprx_tanh),
                                        [pb], [b("uT")]))
        wt, wb = wload(w_in, l, 0, 8, 512, 512)

        def evac_va(j, pt, pb):
            P.op("act", I("activation", out=vg[:TP, :], in_=pt[:TP, :], func=AF.Gelu_apprx_tanh), [pb], [b("vg")])
            P.op("dve", I("tensor_tensor", out=vsq[:TP, :], in0=vg[:TP, :], in1=vg[:TP, :], op=ALU.mult), [b("vg")], [b("vsq")])
            P.op("dve", I("tensor_reduce", out=small[:TP, 8:12], in_=vsq[:TP, :].rearrange("p (g e) -> p g e", g=4),
                                                  axis=AX.X, op=ALU.add), [b("vsq")], [b("small")])
            pow_rstd(small[:TP, 8:12], 128, TP)
            P.op("dve", I("tensor_tensor", out=vsq[:TP, :].rearrange("p (g e) -> p g e", g=4),
                                                  in0=vg[:TP, :].rearrange("p (g e) -> p g e", g=4),
                                                  in1=small[:TP, 8:12].unsqueeze(2).to_broadcast([TP, 4, 128]), op=ALU.mult),
                 [b("vg"), b("small")], [b("vsq")])
            if sample:
                st, sbf = stage()
                P.op("dve", I("tensor_tensor", out=st[:TP, :], in0=vsq[:TP, :], in1=gvr[:TP, :], op=ALU.mult),
                     [b("vsq"), b("gvr")], [sbf])
                P.op("pool", I("tensor_copy", out=vnA[:TP, j, :], in_=st[:TP, :]), [sbf], [b("vnA")])
                store(sguv_s[l, :, :], st[:TP, :], sbf, "sguv_s")
            else:
                P.op("dve", I("tensor_tensor", out=vnA[:TP, j, :], in0=vsq[:TP, :], in1=gvr[:TP, :], op=ALU.mult),
                     [b("vsq"), b("gvr")], [b("vnA")])
        tm_proj(wt, wb, 8, lambda kc, j: hT[:, kc, j * TP:(j + 1) * TP], hb, 512, NS, TP, evac_va)
        wt, wb = wload(w_in, l, 0, 8, 1024, 512)
        fm_proj(wt, wb, 8, [(hp, hp * 128) for hp in range(4)], lambda kc: hT[:, kc, :NT], hb, NT,
                lambda fc, pt, pb: P.op("act", I("activation", out=qk[:, fc, :NT], in_=pt[:, :NT], func=AF.Identity, scale=0.125),
                                        [pb], [b("qk")]))
        wt, wb = wload(w_in, l, 0, 8, 1536, 512)
        fm_proj(wt, wb, 8, [(hp, hp * 128) for hp in range(4)], lambda kc: hT[:, kc, :NT], hb, NT,
                lambda fc, pt, pb: P.op("dve", I("tensor_copy", out=qk[:, 4 + fc, :NT], in_=pt[:, :NT]), [pb], [b("qk")]))

        def evac_k(j, pt, pb):
            st, sbf = stage()
            P.op("act", I("activation", out=st[:TP, :], in_=pt[:TP, :], func=AF.Identity), [pb], [sbf])
            if sample:
                store(sbk_s[l, :, :], st[:TP, :], sbf, "sbk_s")
            else:
                store(sbk_p[l, tok0 + j * 128:tok0 + (j + 1) * 128, :], st[:TP, :], sbf, "sbk_p")
        tm_proj(wt, wb, 8, lambda kc, j: hT[:, kc, j * TP:(j + 1) * TP], hb, 512, NS, TP, evac_k)
        wt, wb = wload(w_in, l, 0, 8, 2048, 512)

        def evac_v(j, pt, pb):
            st, sbf = stage()
            P.op("act", I("activation", out=st[:TP, :], in_=pt[:TP, :], func=AF.Identity), [pb], [sbf])
            P.op("dve", I("tensor_copy", out=vn_b[:TP, j, :], in_=st[:TP, :]), [sbf], [b("vn_b")])
            if sample:
                store(sbv_s[l, :, :], st[:TP, :], sbf, "sbv_s")
            else:
                store(sbv_p[l, tok0 + j * 128:tok0 + (j + 1) * 128, :], st[:TP, :], sbf, "sbv_p")
        tm_proj(wt, wb, 8, lambda kc, j: hT[:, kc, j * TP:(j + 1) * TP], hb, 512, NS, TP, evac_v)
        if not sample and t < n_tiles - 1:
            for hp in range(4):
                P.dma("sp", I("dma_start", out=ktc_d[l, hp, :, tok0:tok0 + TILE], in_=qk[:, 4 + hp, :]),
                      b("qk"), b(f"ktc_d{l}"), b("qk"))
            P.dma("sp", I("dma_start", out=vc_d[l, tok0:tok0 + TILE, :].rearrange("(j p) n -> p j n", p=128), in_=vn_b[:]),
                  b("vn_b"), b(f"vc_d{l}"), b("vn_b"))

        for j in range(NS):
            pt, pb = mmbank()
            for g in range(4):
                if sample:
                    P.op("pe", I("matmul", pt[:, g * 64:(g + 1) * 64], lhsT=vnA[:64, 0, g * 128:(g + 1) * 128],
                                                              rhs=wsTs_sb[:64, l * 4 + g, :], start=(g == 0), stop=True, skip_group_check=True),
                         [b("vnA"), b("wsTs_sb")], [pb])
                else:
                    P.op("pe", I("matmul", pt[:, g * 128:(g + 1) * 128], lhsT=vnA[:, j, g * 128:(g + 1) * 128],
                                                                   rhs=wsT_sb[:, l * 4 + g, :], start=(g == 0), stop=True, skip_group_check=True),
                         [b("vnA"), b("wsT_sb")], [pb])
            if sample:
                P.op("dve", I("tensor_tensor", out=vg[:, 0:256], in0=pt[:, 0:256], in1=bsr_s[:, :], op=ALU.add),
                     [pb, b("bsr_s")], [b("vg")])
                P.op("dve", I("tensor_tensor", out=aT[:, :, 0:64], in0=vg[:, 0:256].rearrange("p (g t) -> p g t", g=4),
                                                      in1=uT[:, :, 0:64], op=ALU.mult), [b("vg"), b("uT")], [b("aT")])
            else:
                P.op("dve", I("tensor_tensor", out=vg[:, :], in0=pt[:, :], in1=bsr[:, :], op=ALU.add),
                     [pb, b("bsr")], [b("vg")])
                P.op("dve", I("tensor_tensor", out=aT[:, :, j * 128:(j + 1) * 128], in0=vg[:, :].rearrange("p (g t) -> p g t", g=4),
                                                           in1=uT[:, :, j * 128:(j + 1) * 128], op=ALU.mult), [b("vg"), b("uT")], [b("aT")])

        blk_i = [0]
        if not sample:
            for hp in range(4):
                s = hp % 2
                if t > 0:
                    P.dma("sp", I("dma_start", out=ktc[s][:, 0:tok0], in_=ktc_d[l, hp, :, 0:tok0]),
                          b(f"ktc_d{l}"), b(f"ktc{s}"), b(f"ktc{s}"))
                    P.dma("sp", I("dma_start",
                        out=vcs[s][:, 0:t * 4, :], in_=vc_d[l, 0:tok0, hp * 128:(hp + 1) * 128].rearrange("(k p) n -> p k n", p=128)),
                        b(f"vc_d{l}"), b(f"vcs{s}"), b(f"vcs{s}"))
                for i2 in range(2):
                    h = hp * 2 + i2
                    pr = slice(i2 * 64, i2 * 64 + 64)
                    P.op("pool", I("memset", s_t[:], 0.0), [], [b("s_t")])
                    blocks = [("d", jj) for jj in (3, 2, 1, 0)] + [("p", kb) for kb in range(t * 4 - 1, -1, -1)]
                    for bi, (kind, idx) in enumerate(blocks):
                        i = blk_i[0] % 2
                        blk_i[0] += 1
                        zt, zb = zp[i], b(f"zp{i}")
                        first = (bi == 0)
                        last = (bi == len(blocks) - 1)
                        if kind == "d":
                            c0 = idx * 128
                            klhs = qk[pr, 4 + hp, idx * 128:(idx + 1) * 128]
                            kbufs = [b("qk")]
                            vsrc = vn_b[:, idx, h * 64:(h + 1) * 64]
                            vbuf = b("vn_b")
                        else:
                            c0 = 0
                            klhs = ktc[s][pr, idx * 128:(idx + 1) * 128]
                            kbufs = [b(f"ktc{s}"), b("qk")]
                            vsrc = vcs[s][:, idx, i2 * 64:(i2 + 1) * 64]
                            vbuf = b(f"vcs{s}")
                        P.op("pe", I("matmul", zt[:, c0:], lhsT=klhs, rhs=qk[pr, hp, c0:TILE],
                                                                              start=True, stop=True), kbufs, [zb])
                        mask_fn = None
                        if kind == "d":
                            def mask_fn(e_, be, c0=c0):
                                P.op("pool", I("tensor_tensor", out=e_[:, c0:c0 + 128], in0=e_[:, c0:c0 + 128], in1=tri_m[:], op=ALU.mult),
                                     [b("tri_m"), be], [be])

                        def av_fn(a_, ba, c0=c0, vsrc=vsrc, vbuf=vbuf, first=first, last=last, pr=pr):
                            P.op("pe", I("matmul", op_[pr, c0:], lhsT=vsrc, rhs=a_[:, c0:], start=first, stop=last, skip_group_check=True),
                                 [vbuf, ba], [b("op_")])
                        sb_block(i, zt, zb, c0, 128, first, mask_fn, av_fn)
                P.op("act", I("activation", out=sq4[:, hp, :], in_=op_[:, :], func=AF.Square), [b("op_")], [b("sq4")])
                P.op("act", I("activation", out=sbTb[:, hp, :], in_=op_[:, :], func=AF.Identity,
                                                          scale=gfm[:, GOFF_GB + l * 4 + hp:GOFF_GB + l * 4 + hp + 1]), [b("op_"), b("gfm")], [b("sbTb")])
        else:
            P.op("pool", I("memset", s_t[:], 0.0), [], [b("s_t")])
            i = blk_i[0] % 2
            blk_i[0] += 1
            zt, zb = zp[i], b(f"zp{i}")
            for i2 in range(2):
                pr = slice(i2 * 64, i2 * 64 + 64)
                for hp in range(4):
                    h = hp * 2 + i2
                    P.op("pe", I("matmul", zt[:64, h * 64:(h + 1) * 64], lhsT=qk[pr, 4 + hp, 0:64], rhs=qk[pr, hp, 0:64],
                                 start=(i2 == 0 and hp == 0), stop=True, skip_group_check=True), [b("qk")], [zb],
                         pe_sync=(i2 == 1 and hp == 0))

            def mask_new(e_, be):
                P.op("pool", I("tensor_tensor", out=e_[:64, :], in0=e_[:64, :], in1=msk_f[0:64, 320:832], op=ALU.mult),
                     [b("msk_f"), be], [be])

            def av_new(a_, ba):
                for h in range(8):
                    hp, i2 = h // 2, h % 2
                    P.op("pe", I("matmul", op_[i2 * 64:i2 * 64 + 64, hp * 64:(hp + 1) * 64],
                                                                     lhsT=vn_b[:64, 0, h * 64:(h + 1) * 64], rhs=a_[:64, h * 64:(h + 1) * 64],
                                                                     start=(h < 2), stop=False, skip_group_check=True), [b("vn_b"), ba], [b("op_")])
            sb_block(i, zt, zb, 0, 64, True, mask_new, av_new)
            for kb in range(7, -1, -1):
                s = kb % 2
                P.dma("pool", I("dma_start", out=kc_tok[s][:], in_=csk[l, :, kb * 128:(kb + 1) * 128, :].rearrange("s p n -> p s n")),
                      b("dram_in"), b(f"kc_tok{s}"), b(f"kc_tok{s}"))
                P.dma("pool", I("dma_start", out=vc_tok[s][:], in_=csv[l, :, kb * 128:(kb + 1) * 128, :].rearrange("s p n -> p s n")),
                      b("dram_in"), b(f"vc_tok{s}"), b(f"vc_tok{s}"))
                for st_ in range(NSTREAM):
                    tp_ap, tb = trbank()
                    for hp in range(4):
                        P.op("pe", I("transpose",
                            out=tp_ap[:, hp * 128:(hp + 1) * 128], in_=kc_tok[s][:, st_, hp * 128:(hp + 1) * 128], identity=ident[:]),
                            [b(f"kc_tok{s}"), b("ident")], [tb])
                    P.op("dve", I("tensor_copy", out=kcT[s][:, st_, :, :].rearrange("p a k -> p (a k)"), in_=tp_ap),
                         [tb], [b(f"kcT{s}")])
                i = blk_i[0] % 2
                blk_i[0] += 1
                zt, zb = zp[i], b(f"zp{i}")
                for i2 in range(2):
                    pr = slice(i2 * 64, i2 * 64 + 64)
                    for st_ in range(NSTREAM):
                        for hp in range(4):
                            h = hp * 2 + i2
                            P.op("pe", I("matmul", zt[:, h * 64 + st_ * 16:h * 64 + st_ * 16 + 16], lhsT=kcT[s][pr, st_, hp, :],
                                         rhs=qk[pr, hp, st_ * 16:st_ * 16 + 16], start=(i2 == 0 and st_ == 0 and hp == 0), stop=True,
                                         skip_group_check=True), [b(f"kcT{s}"), b("qk")], [zb],
                                 pe_sync=(i2 == 1 and st_ == 0 and hp == 0))

                def av_c(a_, ba, s=s, kb=kb):
                    for st_ in range(NSTREAM):
                        for h in range(8):
                            hp, i2 = h // 2, h % 2
                            P.op("pe", I("matmul",
                                op_[i2 * 64:i2 * 64 + 64, hp * 64 + st_ * 16:hp * 64 + st_ * 16 + 16],
                                lhsT=vc_tok[s][:, st_, h * 64:(h + 1) * 64], rhs=a_[:, h * 64 + st_ * 16:h * 64 + st_ * 16 + 16],
                                start=False, stop=(kb == 0), skip_group_check=True), [b(f"vc_tok{s}"), ba], [b("op_")])
                sb_block(i, zt, zb, 0, 128, False, None, av_c)
            P.op("act", I("activation", out=sq4[:, :, 0:64], in_=op_[:, 0:256].rearrange("p (a t) -> p a t", a=4), func=AF.Square),
                 [b("op_")], [b("sq4")])
            for hp in range(4):
                P.op("act", I("activation", out=sbTb[:, hp, 0:64], in_=op_[:, hp * 64:(hp + 1) * 64], func=AF.Identity,
                                                          scale=gfm[:, GOFF_GB + l * 4 + hp:GOFF_GB + l * 4 + hp + 1]), [b("op_"), b("gfm")], [b("sbTb")])

        def feat_rstd(col, square_from=None):
            if square_from is not None:
                srcT, srcb = square_from
                P.op("pool", I("tensor_tensor", out=sq4[:, :, :NT], in0=srcT[:, :, :NT], in1=srcT[:, :, :NT], op=ALU.mult), [srcb], [b("sq4")])
            pt, pb = mmbank()
            for j in range(NS):
                for c in range(4):
                    P.op("pe", I("matmul", pt[:TP, j:j + 1], lhsT=sq4[:, c, j * TP:(j + 1) * TP],
                                                                   rhs=ones_b[:, 0:1], start=(c == 0 and j == 0), stop=(c == 3), skip_group_check=True), [b("sq4"), b("ones_b")], [pb])
            P.op("dve", I("tensor_copy", out=small[:TP, 16 + col:16 + col + NS], in_=pt[:TP, 0:NS]), [pb], [b("small")])
            pow_rstd(small[:TP, 16 + col:16 + col + NS], 512, TP)
        feat_rstd(4)
        feat_rstd(0, (aT, b("aT")))
        for c in range(4):
            P.op("act", I("activation", out=aT[:, c, :NT], in_=aT[:, c, :NT], func=AF.Identity,
                          scale=gfm[:, GOFF_GA + l * 4 + c:GOFF_GA + l * 4 + c + 1]), [b("gfm"), b("aT")], [b("aT")])
        for half in range(2):
            wt, wb = wload(w_out, l, 0, 8, half * 512, 512)
            for (srcT, srcb, kc0, col) in ((aT, b("aT"), 0, 0), (sbTb, b("sbTb"), 4, 4)):
                def evac_m(j, pt, pb, col=col, half=half):
                    P.op("dve", I("scalar_tensor_tensor", out=x_t[:TP, j, half * 512:(half + 1) * 512], in0=pt[:TP, :],
                                                                 scalar=small[:TP, 16 + col + j:16 + col + j + 1],
                                                                 in1=x_t[:TP, j, half * 512:(half + 1) * 512], op0=ALU.mult, op1=ALU.add),
                         [pb, b("small"), bxs[j]], [bxs[j]])
                tm_proj(wt, wb, 4, lambda kc, j, srcT=srcT: srcT[:, kc, j * TP:(j + 1) * TP], [srcb], 512, NS, TP, evac_m, kc0=kc0)

        rms_to_hT(GOFF_LNMEM, l, NS, TP)
        for half in range(2):
            wt, wb = wload(w_mq, l, 0, 8, half * 512, 512)
            fm_proj(wt, wb, 8, [(half * 4 + c, c * 128) for c in range(4)], lambda kc: hT[:, kc, :NT], hb, NT,
                    lambda fc, pt, pb: P.op("act", I("activation", out=qk[:, fc, :NT], in_=pt[:, :NT], func=AF.Identity, scale=1.0 / 16),
                                            [pb], [b("qk")]))
        pti = [0]
        if not sample:
            P.dma("sp", I("dma_start", out=mkT[:], in_=mkT_d[l]), b(f"mkT_d{l}"), b("mkT"), b("mkT"))
            P.dma("sp", I("dma_start", out=mvb[:], in_=mvb_d[l]), b(f"mvb_d{l}"), b("mvb"), b("mvb"))
            for h in range(4):
                den_t, den_b = mmbank()
                o_ts = [mmbank(), (bp[0], b("bp0"))]
                ptile = pT_t[pti[0] % 2]
                ptb = b(f"pT_t{pti[0] % 2}")
                pti[0] += 1
                for kb in range(2):
                    zt, zb = zp[kb], b(f"zp{kb}")
                    for dc in range(2):
                        P.op("pe", I("matmul",
                            zt[:, :], lhsT=mkT[:, 2 * h + dc, kb * 128:(kb + 1) * 128], rhs=qk[:, 2 * h + dc, :],
                            start=(dc == 0), stop=(dc == 1)), [b("mkT"), b("qk")], [zb])
                    P.op("act", I("activation", out=ptile[:, kb, :], in_=zt[:, :], func=AF.Exp), [zb], [ptb])
                for kb in range(2):
                    P.op("pe", I("matmul", den_t[:, :], lhsT=ones_b[:], rhs=ptile[:, kb, :],
                                                                                  start=(kb == 0), stop=(kb == 1)), [b("ones_b"), ptb], [den_b])
                for dc in range(2):
                    ot, ob = o_ts[dc]
                    for kb in range(2):
                        P.op("pe", I("matmul",
                            ot[:, :], lhsT=mvb[:, kb, h * 256 + dc * 128:h * 256 + (dc + 1) * 128], rhs=ptile[:, kb, :],
                            start=(kb == 0), stop=(kb == 1)), [b("mvb"), ptb], [ob])
                P.op("dve", I("reciprocal", out=rden[:, :], in_=den_t[:, :]), [den_b], [b("rden")])
                for dc in range(2):
                    ot, ob = o_ts[dc]
                    P.op("dve", I("tensor_tensor", out=hT[:, 2 * h + dc, :], in0=ot[:, :], in1=rden[:, :], op=ALU.mult),
                         [ob, b("rden")], [b("hT")])
        else:
            den_t, den_b = bp[1], b("bp1")
            ot, ob = bp[0], b("bp0")
            for st_ in range(NSTREAM):
                if st_ % 2 == 0:
                    mk_s, mk_b, mv_s, mv_b = mkT, b("mkT"), mvb, b("mvb")
                else:
                    mk_s, mk_b, mv_s, mv_b = smk1, b("smk1"), smv1, b("smv1")
                P.dma("pool", I("dma_start", out=mk_tok[:], in_=cmk[l, st_].rearrange("(k p) n -> p k n", p=128)),
                      b("dram_in"), b("mk_tok"), b("mk_tok"))
                P.dma("pool", I("dma_start", out=mv_s[:], in_=cmv[l, st_].rearrange("(k p) n -> p k n", p=128)),
                      b("dram_in"), mv_b, mv_b)
                for kb in range(2):
                    for half in range(2):
                        tp_ap, tb = trbank()
                        for cc in range(4):
                            c = half * 4 + cc
                            P.op("pe", I("transpose", out=tp_ap[:, cc * 128:(cc + 1) * 128],
                                                                                            in_=mk_tok[:, kb, c * 128:(c + 1) * 128], identity=ident[:]),
                                 [b("mk_tok"), b("ident")], [tb])
                        P.op("dve", I("tensor_copy",
                            out=mk_s[:, half * 4:half * 4 + 4, kb * 128:(kb + 1) * 128], in_=tp_ap.rearrange("p (c k) -> p c k", c=4)),
                            [tb], [mk_b])
                q0 = st_ * 16
                for h in range(4):
                    ptile = pT_t[pti[0] % 2]
                    ptb = b(f"pT_t{pti[0] % 2}")
                    pti[0] += 1
                    for kb in range(2):
                        zt, zb = zp[kb], b(f"zp{kb}")
                        for dc in range(2):
                            P.op("pe", I("matmul",
                                zt[:, 0:16], lhsT=mk_s[:, 2 * h + dc, kb * 128:(kb + 1) * 128], rhs=qk[:, 2 * h + dc, q0:q0 + 16],
                                start=(dc == 0), stop=(dc == 1)), [mk_b, b("qk")], [zb])
                        P.op("act", I("activation", out=ptile[:, kb, 0:16], in_=zt[:, 0:16], func=AF.Exp), [zb], [ptb])
                    for kb in range(2):
                        P.op("pe", I("matmul", den_t[:, h * 64 + q0:h * 64 + q0 + 16], lhsT=ones_b[:], rhs=ptile[:, kb, 0:16],
                                     start=(kb == 0 and h == 0 and st_ == 0), stop=(kb == 1), skip_group_check=True), [b("ones_b"), ptb], [den_b])
                    for dc in range(2):
                        for kb in range(2):
                            P.op("pe", I("matmul",
                                ot[:, dc * 256 + h * 64 + q0:dc * 256 + h * 64 + q0 + 16], lhsT=mv_s[:, kb, h * 256 + dc * 128:h * 256 + (dc + 1) * 128],
                                rhs=ptile[:, kb, 0:16], start=(kb == 0 and dc == 0 and h == 0 and st_ == 0), stop=(kb == 1), skip_group_check=True),
                                [mv_b, ptb], [ob])
            P.op("dve", I("reciprocal", out=rden[:, 0:256], in_=den_t[:, 0:256]), [den_b], [b("rden")])
            for h in range(4):
                for dc in range(2):
                    P.op("dve", I("tensor_tensor", out=hT[:, 2 * h + dc, 0:64], in0=ot[:, dc * 256 + h * 64:dc * 256 + (h + 1) * 64],
                                                                      in1=rden[:, h * 64:(h + 1) * 64], op=ALU.mult), [ob, b("rden")], [b("hT")])
        for half in range(2):
            wt, wb = wload(w_mo, l, 0, 8, half * 512, 512)

            def evac_o(j, pt, pb, half=half):
                P.op("dve", I("tensor_tensor", out=x_t[:TP, j, half * 512:(half + 1) * 512], in0=pt[:TP, :],
                                                      in1=x_t[:TP, j, half * 512:(half + 1) * 512], op=ALU.add), [pb, bxs[j]], [bxs[j]])
            tm_proj(wt, wb, 8, lambda kc, j: hT[:, kc, j * TP:(j + 1) * TP], hb, 512, NS, TP, evac_o)

        rms_to_hT(GOFF_LNFFN, l, NS, TP)
        for (kb0, nkc) in ((0, 8), (8, 8), (16, 6)):
            for c0 in range(0, nkc * 128, 512):
                ncol = min(512, nkc * 128 - c0)
                wtg, wbg = wload(w_g, l, 0, 8, kb0 * 128 + c0, ncol)
                wtu, wbu = wload(w_u, l, 0, 8, kb0 * 128 + c0, ncol)
                for c in range(ncol // 128):
                    fc = c0 // 128 + c
                    pg, pgb = mmbank()
                    for kc in range(8):
                        P.op("pe", I("matmul", pg[:, :NT], lhsT=wtg[:, kc, c * 128:(c + 1) * 128], rhs=hT[:, kc, :NT],
                                                                                 start=(kc == 0), stop=(kc == 7)), [wbg, b("hT")], [pgb])
                    P.op("act", I("activation", out=vg[:, :NT], in_=pg[:, :NT], func=AF.Silu), [pgb], [b("vg")])
                    pu, pub = mmbank()
                    for kc in range(8):
                        P.op("pe", I("matmul", pu[:, :NT], lhsT=wtu[:, kc, c * 128:(c + 1) * 128], rhs=hT[:, kc, :NT],
                                                                                 start=(kc == 0), stop=(kc == 7)), [wbu, b("hT")], [pub])
                    P.op("dve", I("tensor_tensor", out=actT[:, fc, :NT], in0=pu[:, :NT], in1=vg[:, :NT], op=ALU.mult),
                         [pub, b("vg")], [b("actT")])
            for half in range(2):
                wt, wb = wload(w_d, l, kb0 * 128, nkc, half * 512, 512)

                def evac_d(j, pt, pb, half=half):
                    P.op("dve", I("tensor_tensor", out=x_t[:TP, j, half * 512:(half + 1) * 512], in0=pt[:TP, :],
                                                          in1=x_t[:TP, j, half * 512:(half + 1) * 512], op=ALU.add), [pb, bxs[j]], [bxs[j]])
                tm_proj(wt, wb, nkc, lambda kc, j: actT[:, kc, j * TP:(j + 1) * TP], [b("actT")], 512, NS, TP, evac_d)

    def final_norm(t, sample):
        NS, TP = (1, 64) if sample else (4, 128)
        ld("sp", gvr[:], lnf_rep[:, 0:512], "gvr")
        ld("sp", bsr[:], lnf_rep[:, 512:1024], "bsr")
        lnh = [(gvr, b("gvr")), (bsr, b("bsr"))]
        for j in range(NS):
            bx = b(f"x{j}")
            P.op("act", I("activation", out=sqs[:TP, :], in_=x_t[:TP, j, :], func=AF.Square), [bx], [b("sqs")])
            P.op("dve", I("tensor_reduce", out=small[:TP, j:j + 1], in_=sqs[:TP, :], axis=AX.X, op=ALU.add), [b("sqs")], [b("small")])
            pow_rstd(small[:TP, j:j + 1], D, TP)
            for half in range(2):
                st, sbf = stage()
                P.op("dve", I("scalar_tensor_tensor",
                    out=st[:TP, :], in0=x_t[:TP, j, half * 512:(half + 1) * 512], scalar=small[:TP, j:j + 1],
                    in1=lnh[half][0][:TP, :], op0=ALU.mult, op1=ALU.mult), [bx, b("small"), lnh[half][1]], [sbf])
                if sample:
                    store(y_s[:, half * 512:(half + 1) * 512], st[:TP, :], sbf, "y_s")
                else:
                    store(y_p[t * TILE + j * 128:t * TILE + (j + 1) * 128, half * 512:(half + 1) * 512], st[:TP, :], sbf, "y_p")

    def mem_prologue():
        for kb in range(2):
            P.dma("pool", I("dma_start", out=mk_tok[:, kb, :], in_=memp[kb * 128:(kb + 1) * 128, :]),
                  b("dram_in"), b("mk_tok"), b("mk_tok"))
        for kb in range(2):
            for half in range(2):
                tp_ap, tb = trbank()
                for cc in range(4):
                    c = half * 4 + cc
                    P.op("pe", I("transpose", out=tp_ap[:, cc * 128:(cc + 1) * 128],
                                                                                    in_=mk_tok[:, kb, c * 128:(c + 1) * 128], identity=ident[:]),
                         [b("mk_tok"), b("ident")], [tb])
                P.op("dve", I("tensor_copy",
                    out=memT[:, half * 4:half * 4 + 4, kb * 128:(kb + 1) * 128], in_=tp_ap.rearrange("p (c k) -> p c k", c=4)), [tb], [b("memT")])
        for l in range(depth):
            for (wsrc, outd, is_k) in ((w_mk, mk_p, True), (w_mv, mv_p, False)):
                for half in range(2):
                    wt, wb = wload(wsrc, l, 0, 8, half * 512, 512)
                    for kb in range(2):
                        pt, pb = mmbank()
                        for kc in range(8):
                            P.op("pe", I("matmul", pt[:, :], lhsT=memT[:, kc, kb * 128:(kb + 1) * 128], rhs=wt[:, kc, :],
                                                                                     start=(kc == 0), stop=(kc == 7)), [wb, b("memT")], [pb])
                        st, sbf = stage()
                        P.op("act", I("activation", out=st[:, :], in_=pt[:, :], func=AF.Identity), [pb], [sbf])
                        store(outd[l, kb * 128:(kb + 1) * 128, half * 512:(half + 1) * 512], st[:, :], sbf, "mkv_out")
                        if not is_k:
                            P.op("dve", I("tensor_copy", out=mvb[:, kb, half * 512:(half + 1) * 512], in_=st[:, :]),
                                 [sbf], [b("mvb")])
                    if is_k:
                        for c in range(4):
                            fc = half * 4 + c
                            pt, pb = mmbank()
                            for kc in range(8):
                                P.op("pe", I("matmul", pt[:, :NMEM], lhsT=wt[:, kc, c * 128:(c + 1) * 128], rhs=memT[:, kc, :],
                                                                                       start=(kc == 0), stop=(kc == 7)), [wb, b("memT")], [pb])
                            P.op("dve", I("tensor_copy", out=mkT[:, fc, :], in_=pt[:, :NMEM]), [pb], [b("mkT")])
            P.dma("sp", I("dma_start", out=mkT_d[l], in_=mkT[:]), b("mkT"), b(f"mkT_d{l}"), b("mkT"))
            P.dma("sp", I("dma_start", out=mvb_d[l], in_=mvb[:]), b("mvb"), b(f"mvb_d{l}"), b("mvb"))

    mem_prologue()
    for t in range(n_tiles):
        for j in range(4):
            P.dma("sp", I("dma_start", out=x_t[:, j, :], in_=xp[t * TILE + j * 128:t * TILE + (j + 1) * 128, :]),
                  b("dram_in"), b(f"x{j}"), b(f"x{j}"))
        for l in range(depth):
            layer(l, t, False)
        final_norm(t, False)
    if with_sample:
        P.dma("sp", I("dma_start", out=x_t[:64, 0, :], in_=xs[:, :]), b("dram_in"), b("x0"), b("x0"))
        for l in range(depth):
            layer(l, 0, True)
        final_norm(0, True)
    P.finish()

    sems = {}
    for k in P.semkeys:
        sems[k] = es.enter_context(nc.semaphore("s_" + "_".join(str(v) for v in k)))
    engmap = {"pe": "tensor", "act": "scalar", "dve": "vector", "pool": "gpsimd", "sp": "sync"}
    with es:
        with nc.Block() as block:
            def make(engname):
                def body(eng):
                    for (waits, fn, inc) in P.q[engname]:
                        for (k, v) in waits:
                            eng.wait_ge(sems[k], v)
                        if fn is not None:
                            ins = fn(eng)
                            ins.then_inc(sems[inc[0]], inc[1])
                return body
            for en in ENGS:
                getattr(block, engmap[en])(make(en))
    counts = {e: len(P.q[e]) for e in ENGS}
    return nc, counts


def _host_consts(inp):
    L = DEPTH
    f = np.float32

    def fm(v, nch):
        return np.ascontiguousarray(np.asarray(v, f).reshape(nch, 128).T)
    cols = []
    for name in ("ln_mix", "ln_mem", "ln_ffn"):
        for l in range(L):
            cols.append(fm(inp[name][l], 8))
    cols.append(fm(inp["ln_final"], 8))
    for name in ("g_out_sgu", "g_out_sb"):
        for l in range(L):
            cols.append(fm(inp[name][l], 4))
    gains_fm = np.ascontiguousarray(np.concatenate(cols, axis=1))
    gv = np.asarray(inp["g_sgu_v"], f).reshape(1, L * 512)
    gv_rep = np.ascontiguousarray(np.broadcast_to(gv, (128, L * 512)))
    bs = np.asarray(inp["b_sgu"], f)
    bs_rep = np.ascontiguousarray(np.broadcast_to(bs.reshape(1, L * 512), (128, L * 512)))
    bs_s = np.tile(bs[:, :, :16], (1, 1, 4))
    bs_rep_s = np.ascontiguousarray(np.broadcast_to(bs_s.reshape(1, L * 256), (128, L * 256)))
    ws = np.asarray(inp["w_sgu"], f)
    wsT = np.ascontiguousarray(ws.transpose(0, 1, 3, 2))
    wsT_s = np.zeros((L, 4, 64, 64), f)
    for s in range(4):
        wsT_s[:, :, s * 16:(s + 1) * 16, s * 16:(s + 1) * 16] = wsT[:, :, :16, :16]
    k = np.arange(128)
    tri = (k[:, None] < k[None, :]).astype(f)
    negtri = -(k[:, None] >= k[None, :]).astype(f)
    k64 = np.arange(64)
    same = (k64[:, None] // 16) == (k64[None, :] // 16)
    bd = (same & (k64[:, None] <= k64[None, :])).astype(f)
    newm = (same & (k64[:, None] < k64[None, :])).astype(f)
    newmask = np.zeros((128, 512), f)
    newmask[:64, :] = np.tile(newm, (1, 8))
    m = np.zeros((128, 960), f)
    m[:, 832:960] = np.eye(128, dtype=f)
    m[:, 0:128] = tri
    m[:, 128:256] = negtri
    m[:64, 256:320] = bd
    m[:, 320:832] = newmask
    lnf_rep = np.ascontiguousarray(np.broadcast_to(np.asarray(inp["ln_final"], f).reshape(1, D), (128, D)))
    return dict(gains_fm=gains_fm, gv_rep=gv_rep, bs_rep=bs_rep, bs_rep_s=bs_rep_s, wsT=wsT, wsT_s=wsT_s, masks=m, lnf_rep=lnf_rep)


_CACHE = {}


def kernel(**inputs):
    inp = {k: np.asarray(v) for k, v in inputs.items()}
    n_tiles = int(inp.pop("_n_tiles", SEQ // TILE)) if "_n_tiles" in inp else SEQ // TILE
    depth = int(inp.pop("_depth", DEPTH)) if "_depth" in inp else DEPTH
    key = (n_tiles, depth)
    if key not in _CACHE:
        _CACHE[key] = build(n_tiles, depth, os.environ.get("KNOSAMPLE") is None)
    nc, counts = _CACHE[key]
    consts = _host_consts(inp)
    f = np.float32
    shared = {
        "w_in": inp["w_in"], "w_out": inp["w_out"], "w_mq": inp["w_mq"], "w_mk": inp["w_mk"], "w_mv": inp["w_mv"],
        "w_mo": inp["w_mo"], "w_g": inp["w_ffn_gate"], "w_u": inp["w_ffn_up"], "w_d": inp["w_ffn_down"],
    }
    shared = {k: np.ascontiguousarray(v, dtype=f) for k, v in shared.items()}
    shared.update(consts)
    in_maps = []
    for c in range(8):
        bq = c // 2
        s0 = c * NSTREAM
        m = dict(shared)
        m["xp"] = np.ascontiguousarray(inp["x_prompt"][bq], dtype=f)
        m["xs"] = np.ascontiguousarray(inp["x_sample"][s0:s0 + NSTREAM].reshape(64, D), dtype=f)
        m["csk"] = np.ascontiguousarray(inp["cache_sb_k"][:, s0:s0 + NSTREAM].reshape(DEPTH, NSTREAM, PAST, 512), dtype=f)
        m["csv"] = np.ascontiguousarray(inp["cache_sb_v"][:, s0:s0 + NSTREAM].reshape(DEPTH, NSTREAM, PAST, 512), dtype=f)
        m["cmk"] = np.ascontiguousarray(inp["cache_mem_k"][:, s0:s0 + NSTREAM].reshape(DEPTH, NSTREAM, NMEM, D), dtype=f)
        m["cmv"] = np.ascontiguousarray(inp["cache_mem_v"][:, s0:s0 + NSTREAM].reshape(DEPTH, NSTREAM, NMEM, D), dtype=f)
        m["memp"] = np.ascontiguousarray(inp["mem_prompt"][bq], dtype=f)
        in_maps.append(m)
    ncores = int(os.environ.get("KCORES", "8"))
    res = run_bass_kernel_spmd(nc, in_maps[:ncores], core_ids=list(range(ncores)))
    R = list(res.results) + [res.results[0]] * (8 - ncores)
    B4 = 4

    def g(c, name):
        return np.asarray(R[c][name], dtype=f)
    y_prompt = np.stack([g(2 * bq, "y_p") for bq in range(B4)])
    y_sample = np.concatenate([g(c, "y_s").reshape(NSTREAM, DSEQ, D) for c in range(8)], axis=0)
    sb_k_prompt = np.stack([g(2 * bq, "sbk_p") for bq in range(B4)], axis=1).reshape(DEPTH, B4, SEQ, 8, 64)
    sb_v_prompt = np.stack([g(2 * bq, "sbv_p") for bq in range(B4)], axis=1).reshape(DEPTH, B4, SEQ, 8, 64)
    mem_k_prompt = np.stack([g(2 * bq, "mk_p") for bq in range(B4)], axis=1).reshape(DEPTH, B4, NMEM, 4, 256)
    mem_v_prompt = np.stack([g(2 * bq, "mv_p") for bq in range(B4)], axis=1).reshape(DEPTH, B4, NMEM, 4, 256)
    sb_k_sample = np.concatenate([g(c, "sbk_s").reshape(DEPTH, NSTREAM, DSEQ, 8, 64) for c in range(8)], axis=1)
    sb_v_sample = np.concatenate([g(c, "sbv_s").reshape(DEPTH, NSTREAM, DSEQ, 8, 64) for c in range(8)], axis=1)
    sgu_v_sample = np.concatenate([g(c, "sguv_s").reshape(DEPTH, NSTREAM, DSEQ, 4, 128) for c in range(8)], axis=1)
    return (y_prompt, y_sample, sb_k_prompt, sb_v_prompt, mem_k_prompt, mem_v_prompt, sb_k_sample, sb_v_sample, sgu_v_sample)
```

```python
import contextlib
import os
import numpy as np
import concourse.bass as bass
import concourse.mybir as mybir
from concourse.bass_utils import run_bass_kernel_spmd

F32 = mybir.dt.float32
BF16 = mybir.dt.bfloat16
AF = mybir.ActivationFunctionType
ALU = mybir.AluOpType
AX = mybir.AxisListType

D = 1024
SEQ = 4096
DEPTH = 4
NSTREAM = 4
DSEQ = 16
PAST = 1024
NMEM = 256
DFF = 2816
NFC = DFF // 128
EPS = 1e-6
TILE = 512

ENGS = ("pe", "act", "dve", "pool", "sp")


def I(meth, *a, **k):
    f = lambda e: getattr(e, meth)(*a, **k)
    f.desc = (meth, a, k)
    return f


PSUM_BUFS = {"mm0", "mm1", "zp0", "zp1", "bp0", "bp1", "op_", "trp"}


class Buf:
    __slots__ = ("name", "lw", "rd", "dma_tot")

    def __init__(self, name):
        self.name = name
        self.lw = None
        self.rd = {}
        self.dma_tot = 0


class Prog:
    CH = 12000

    def __init__(self):
        self.q = {e: [] for e in ENGS}
        self.n = {e: 0 for e in ENGS}
        self.waited = {e: {} for e in ENGS}
        self.semkeys = {}
        self.out_events = {}
        self.nrec = 0
        self.kstop = int(os.environ.get("KSTOP", "1000000000"))

    def _wait_list(self, eng, deps, pe_sync=False):
        waits = []
        for (key, val, src) in deps:
            if src == eng and eng == "pe" and not pe_sync:
                continue
            if self.waited[eng].get(key, 0) >= val:
                continue
            self.waited[eng][key] = val
            waits.append((key, val))
        return waits

    def op(self, eng, fn, reads=(), writes=(), pe_sync=False):
        self.nrec += 1
        if self.nrec > self.kstop:
            return
        deps = []
        for b in reads:
            if b.lw is not None:
                deps.append(b.lw)
            if b.name in PSUM_BUFS:
                deps.extend(ev for ev in b.rd.values() if ev[2] != eng)
        for b in writes:
            if b.lw is not None:
                deps.append(b.lw)
            deps.extend(b.rd.values())
        i = self.n[eng]
        self.n[eng] += 1
        key = ("e", eng, i // self.CH)
        ev = (key, i % self.CH + 1, eng)
        self.semkeys[key] = True
        waits = self._wait_list(eng, deps, pe_sync)
        self.q[eng].append((waits, fn, (key, 1)))
        for b in writes:
            b.lw = ev
            b.rd = {}
        for b in reads:
            if b not in writes:
                b.rd[key] = ev

    def dma(self, eng, fn, src, dst, side, is_output=False):
        self.nrec += 1
        if self.nrec > self.kstop:
            return
        deps = []
        if src.lw is not None:
            deps.append(src.lw)
        if dst.lw is not None:
            deps.append(dst.lw)
        deps.extend(dst.rd.values())
        key = ("d", side.name, eng)
        self.semkeys[key] = True
        if not isinstance(side.dma_tot, dict):
            side.dma_tot = {}
        tot = side.dma_tot.get(key, 0)
        if tot > 0:
            deps.append((key, tot, "dma"))
        tot += 16
        side.dma_tot[key] = tot
        ev = (key, tot, "dma")
        waits = self._wait_list(eng, deps)
        self.q[eng].append((waits, fn, (key, 16)))
        dst.lw = ev
        dst.rd = {}
        src.rd[key] = ev
        if is_output:
            self.out_events[key] = ev

    def finish(self):
        waits = self._wait_list("sp", list(self.out_events.values()))
        self.q["sp"].append((waits, None, None))


def build(n_tiles=SEQ // TILE, depth=DEPTH, with_sample=True):
    nc = bass.Bass("TRN2", target_bir_lowering=False)
    P = Prog()
    L = DEPTH

    def din(name, shape, dt=F32):
        return nc.dram_tensor(name, list(shape), dt, kind="ExternalInput").ap()

    def dout(name, shape, dt=F32):
        return nc.dram_tensor(name, list(shape), dt, kind="ExternalOutput").ap()

    xp = din("xp", [SEQ, D])
    xs = din("xs", [64, D])
    csk = din("csk", [L, NSTREAM, PAST, 512])
    csv = din("csv", [L, NSTREAM, PAST, 512])
    cmk = din("cmk", [L, NSTREAM, NMEM, D])
    cmv = din("cmv", [L, NSTREAM, NMEM, D])
    memp = din("memp", [NMEM, D])
    w_in = din("w_in", [L, D, 2560])
    w_out = din("w_out", [L, D, D])
    w_mq = din("w_mq", [L, D, D])
    w_mk = din("w_mk", [L, D, D])
    w_mv = din("w_mv", [L, D, D])
    w_mo = din("w_mo", [L, D, D])
    w_g = din("w_g", [L, D, DFF])
    w_u = din("w_u", [L, D, DFF])
    w_d = din("w_d", [L, DFF, D])
    gains_fm = din("gains_fm", [128, 3 * L * 8 + 8 + 2 * L * 4])
    gv_rep = din("gv_rep", [128, L * 512])
    bs_rep = din("bs_rep", [128, L * 512])
    bs_rep_s = din("bs_rep_s", [128, L * 256])
    wsT = din("wsT", [L, 4, 128, 128])
    wsT_s = din("wsT_s", [L, 4, 64, 64])
    masks = din("masks", [128, 960])
    lnf_rep = din("lnf_rep", [128, D])

    y_p = dout("y_p", [SEQ, D])
    y_s = dout("y_s", [64, D])
    sbk_p = dout("sbk_p", [L, SEQ, 512])
    sbv_p = dout("sbv_p", [L, SEQ, 512])
    mk_p = dout("mk_p", [L, NMEM, D])
    mv_p = dout("mv_p", [L, NMEM, D])
    sbk_s = dout("sbk_s", [L, 64, 512])
    sbv_s = dout("sbv_s", [L, 64, 512])
    sguv_s = dout("sguv_s", [L, 64, 512])

    ktc_d = nc.dram_tensor("ktc_d", [L, 4, 128, SEQ], BF16).ap()
    vc_d = nc.dram_tensor("vc_d", [L, SEQ, 512], BF16).ap()
    mkT_d = nc.dram_tensor("mkT_d", [L, 128, 8, NMEM], BF16).ap()
    mvb_d = nc.dram_tensor("mvb_d", [L, 128, 2, D], BF16).ap()

    es = contextlib.ExitStack()

    def sb(name, shape, dt):
        return es.enter_context(nc.sbuf_tensor(name, list(shape), dt))

    def ps(name, shape, dt=F32):
        return es.enter_context(nc.psum_tensor(name, list(shape), dt))

    x_t = sb("x_t", [128, 4, D], F32)
    hT = sb("hT", [128, 8, TILE], BF16)
    xn = sb("xn", [128, D], BF16)
    sqs = sb("sqs", [128, D], F32)
    NW = 3
    wr = [sb(f"wr{i}", [128, 8, 512], BF16) for i in range(NW)]
    qk = sb("qk", [128, 8, TILE], BF16)
    vn_b = sb("vn_b", [128, 4, 512], BF16)
    stg = [sb(f"stg{i}", [128, 512], F32) for i in range(3)]
    ktc = [sb(f"ktc{i}", [128, SEQ], BF16) for i in range(2)]
    vcs = [sb(f"vcs{i}", [128, (SEQ - TILE) // 128, 128], BF16) for i in range(2)]
    e_t = [sb(f"e_t{i}", [128, 512], F32) for i in range(2)]
    lk_t = [sb(f"lk_t{i}", [128, 512], BF16) for i in range(2)]
    p_t = [sb(f"p_t{i}", [128, 512], BF16) for i in range(2)]
    a_t = [sb(f"a_t{i}", [128, 512], BF16) for i in range(2)]
    s_t = sb("s_t", [128, 512], BF16)
    uT = sb("uT", [128, 4, TILE], BF16)
    vg = sb("vg", [128, 512], F32)
    vsq = sb("vsq", [128, 512], F32)
    vnA = sb("vnA", [128, 4, 512], BF16)
    aT = sb("aT", [128, 4, TILE], BF16)
    sbTb = sb("sbTb", [128, 4, TILE], BF16)
    sq4 = sb("sq4", [128, 4, TILE], BF16)
    actT = sb("actT", [128, 8, TILE], BF16)
    mkT = sb("mkT", [128, 8, NMEM], BF16)
    mvb = sb("mvb", [128, 2, D], BF16)
    pT_t = [sb(f"pT_t{i}", [128, 2, TILE], BF16) for i in range(2)]
    rden = sb("rden", [128, TILE], F32)
    small = sb("small", [128, 64], F32)
    gfm = sb("gfm", [128, 3 * L * 8 + 8 + 2 * L * 4], F32)
    gvr = sb("gvr", [128, 512], F32)
    bsr = sb("bsr", [128, 512], F32)
    bsr_s = sb("bsr_s", [128, 256], F32)
    wsT_sb = sb("wsT_sb", [128, L * 4, 128], BF16)
    wsTs_sb = sb("wsTs_sb", [64, L * 4, 64], BF16)
    msk_f = sb("msk_f", [128, 960], F32)
    tri_m = sb("tri_m", [128, 128], F32)
    negtri = sb("negtri", [128, 128], BF16)
    negones = sb("negones", [128, 128], BF16)
    ones_b = sb("ones_b", [128, 128], BF16)
    ident = sb("ident", [128, 128], BF16)
    bd_m = sb("bd_m", [64, 64], BF16)
    kc_tok = [ktc[i][:, 0:2048].rearrange("p (s n) -> p s n", s=4) for i in range(2)]
    kcT = [ktc[i][:, 2048:4096].rearrange("p (s a k) -> p s a k", s=4, a=4) for i in range(2)]
    vc_tok = [vcs[i][:, 0:16, :].rearrange("p (s a) k -> p s (a k)", s=4) for i in range(2)]
    mk_tok = sb("mk_tok", [128, 2, D], BF16)
    memT = actT[:, 0:4, :].rearrange("p a (b k) -> p (a b) k", b=2)
    smk1 = sb("smk1", [128, 8, NMEM], BF16)
    smv1 = sb("smv1", [128, 2, D], BF16)

    mm = [ps(f"mm{i}", [128, 512]) for i in range(2)]
    zp = [ps(f"zp{i}", [128, 512]) for i in range(2)]
    bp = [ps(f"bp{i}", [128, 512]) for i in range(2)]
    op_ = ps("op_", [128, 512])
    trp = ps("trp", [128, 1024], BF16)

    B = {}

    ALIAS = {"kc_tok0": "ktc0", "kc_tok1": "ktc1", "kcT0": "ktc0", "kcT1": "ktc1", "vc_tok0": "vcs0", "vc_tok1": "vcs1",
             "memT": "actT"}

    def b(name):
        name = ALIAS.get(name, name)
        if name not in B:
            B[name] = Buf(name)
        return B[name]

    GOFF_LNMIX, GOFF_LNMEM, GOFF_LNFFN = 0, L * 8, 2 * L * 8
    GOFF_LNF = 3 * L * 8
    GOFF_GA = GOFF_LNF + 8
    GOFF_GB = GOFF_GA + L * 4

    def ld(eng, dst_ap, src_ap, dname, sname="dram_in"):
        P.dma(eng, I("dma_start", out=dst_ap, in_=src_ap), b(sname), b(dname), b(dname))

    ld("sp", gfm[:], gains_fm[:, :], "gfm")
    ld("sp", msk_f[:], masks[:, :], "msk_f")
    ld("pool", wsT_sb[:], wsT.rearrange("l g s t -> s (l g) t"), "wsT_sb")
    ld("pool", wsTs_sb[:], wsT_s.rearrange("l g s t -> s (l g) t"), "wsTs_sb")
    P.op("dve", I("tensor_copy", out=tri_m[:], in_=msk_f[:, 0:128]), [b("msk_f")], [b("tri_m")])
    P.op("dve", I("tensor_copy", out=negtri[:], in_=msk_f[:, 128:256]), [b("msk_f")], [b("negtri")])
    P.op("dve", I("tensor_copy", out=bd_m[:], in_=msk_f[0:64, 256:320]), [b("msk_f")], [b("bd_m")])
    epsb = sb("epsb", [128, 1], F32)
    P.op("pool", I("memset", epsb[:], EPS), [], [b("epsb")])
    P.op("pool", I("memset", negones[:], -1.0), [], [b("negones")])
    P.op("pool", I("memset", ones_b[:], 1.0), [], [b("ones_b")])
    P.op("dve", I("tensor_copy", out=ident[:], in_=msk_f[:, 832:960]), [b("msk_f")], [b("ident")])
    incl = sb("incl", [128, 128], BF16)
    P.op("dve", I("tensor_tensor", out=incl[:], in0=tri_m[:], in1=ident[:], op=ALU.add),
         [b("tri_m"), b("ident")], [b("incl")])
    for i in range(L * 4):
        P.op("pool", I("tensor_tensor", out=wsT_sb[:, i, :], in0=wsT_sb[:, i, :], in1=incl[:], op=ALU.mult),
             [b("incl"), b("wsT_sb")], [b("wsT_sb")])
        P.op("pool", I("tensor_tensor", out=wsTs_sb[:, i, :], in0=wsTs_sb[:, i, :], in1=bd_m[:], op=ALU.mult),
             [b("bd_m"), b("wsTs_sb")], [b("wsTs_sb")])

    wstate = {"i": 0}

    def wload(w_ap, l, r0, nkc, c0, ncols):
        s = wstate["i"] % NW
        wstate["i"] += 1
        t = wr[s]
        src = w_ap[l, r0:r0 + nkc * 128, c0:c0 + ncols].rearrange("(kc p) n -> p kc n", p=128)
        P.dma("pool", I("dma_start", out=t[:, 0:nkc, 0:ncols], in_=src), b("dram_in"), b(f"wr{s}"), b(f"wr{s}"))
        return t, b(f"wr{s}")

    mmstate = {"i": 0}

    MMROT = [(mm[0], "mm0"), (mm[1], "mm1"), (zp[0], "zp0"), (zp[1], "zp1"), (bp[1], "bp1")]

    def mmbank():
        t_, n_ = MMROT[mmstate["i"] % len(MMROT)]
        mmstate["i"] += 1
        return t_, b(n_)

    stgstate = {"i": 0}

    def stage():
        i = stgstate["i"] % 3
        stgstate["i"] += 1
        return stg[i], b(f"stg{i}")

    trstate = {"i": 0}

    mm1_bf = mm[1][:].bitcast(BF16)

    def trbank():
        i = trstate["i"] % 2
        trstate["i"] += 1
        return trp[:, i * 512:(i + 1) * 512], b("trp")

    def store(dst_ap, src_ap, sbuf_buf, out_name):
        P.dma("sp", I("dma_start", out=dst_ap, in_=src_ap), sbuf_buf, b(out_name), sbuf_buf, is_output=True)

    def rms_to_hT(goff, l, NS, TP):
        NT = NS * TP
        for j in range(NS):
            bx = b(f"x{j}")
            P.op("act", I("activation", out=sqs[:TP, :], in_=x_t[:TP, j, :], func=AF.Square),
                 [bx], [b("sqs")])
            P.op("dve", I("tensor_reduce", out=small[:TP, j:j + 1], in_=sqs[:TP, :], axis=AX.X, op=ALU.add),
                 [b("sqs")], [b("small")])
            pow_rstd(small[:TP, j:j + 1], D, TP)
            P.op("dve", I("tensor_scalar", out=xn[:TP, :], in0=x_t[:TP, j, :], scalar1=small[:TP, j:j + 1],
                                                       scalar2=None, op0=ALU.mult), [bx, b("small")], [b("xn")])
            for half in range(2):
                tp_ap, tb = trbank()
                for cc in range(4):
                    c = half * 4 + cc
                    P.op("pe", I("transpose",
                        out=tp_ap[:, cc * 128:cc * 128 + TP], in_=xn[:TP, c * 128:(c + 1) * 128], identity=ident[:TP, :TP]),
                        [b("xn"), b("ident")], [tb])
                for cc in range(4):
                    c = half * 4 + cc
                    P.op("act", I("activation",
                        out=hT[:, c, j * TP:(j + 1) * TP], in_=tp_ap[:, cc * 128:cc * 128 + TP], func=AF.Identity,
                        scale=gfm[:, goff + l * 8 + c:goff + l * 8 + c + 1]), [tb, b("gfm")], [b("hT")])

    def fm_proj(wt, wb, nkc, fcs, rhs_fn, rhs_bufs, NT, evac):
        for (fc, col0) in fcs:
            pt, pb = mmbank()
            for kc in range(nkc):
                P.op("pe", I("matmul",
                    pt[:, :NT], lhsT=wt[:, kc, col0:col0 + 128], rhs=rhs_fn(kc), start=(kc == 0), stop=(kc == nkc - 1)),
                    [wb] + rhs_bufs, [pb])
            evac(fc, pt, pb)

    def tm_proj(wt, wb, nkc, lhs_fn, lhs_bufs, ncols, NS, TP, evac, kc0=0):
        for j in range(NS):
            pt, pb = mmbank()
            for kc in range(nkc):
                P.op("pe", I("matmul",
                    pt[:TP, :ncols], lhsT=lhs_fn(kc, j), rhs=wt[:, kc0 + kc, 0:ncols], start=(kc == 0), stop=(kc == nkc - 1)),
                    [wb] + lhs_bufs, [pb])
            evac(j, pt, pb)

    def pow_rstd(ap, n, TP):
        P.op("act", I("activation", out=ap, in_=ap, func=AF.Ln, scale=1.0 / n, bias=epsb[:TP, 0:1]), [b("small"), b("epsb")], [b("small")])
        P.op("act", I("activation", out=ap, in_=ap, func=AF.Exp, scale=-0.5), [b("small")], [b("small")])

    def sb_stage_a(i, zt, zb, c0, kp, mask_fn):
        e_, lk_ = e_t[i], lk_t[i]
        be, blk = b(f"e{i}"), b(f"lk{i}")
        P.op("act", I("activation", out=e_[:kp, c0:], in_=zt[:kp, c0:], func=AF.Exp), [zb], [be])
        if mask_fn is not None:
            mask_fn(e_, be)
        P.op("act", I("activation", out=lk_[:kp, c0:], in_=e_[:kp, c0:], func=AF.Ln, bias=1.0, scale=1.0), [be], [blk])

    def sb_stage_b(i, c0, kp, first, av_fn):
        e_, lk_, p_, a_ = e_t[i], lk_t[i], p_t[i], a_t[i]
        be, blk, bpp, ba = b(f"e{i}"), b(f"lk{i}"), b(f"p{i}"), b(f"a{i}")
        bt, bb = bp[i], b(f"bp{i}")
        P.op("pe", I("matmul", bt[:kp, c0:], lhsT=negtri[:kp, :kp], rhs=lk_[:kp, c0:], start=True, stop=first),
             [b("negtri"), blk], [bb])
        if not first:
            P.op("pe", I("matmul", bt[:kp, c0:], lhsT=negones[:, :kp], rhs=s_t[:, c0:], start=False, stop=True),
                 [b("negones"), b("s_t")], [bb])
        P.op("act", I("activation", out=p_[:kp, c0:], in_=bt[:kp, c0:], func=AF.Exp), [bb], [bpp])
        P.op("dve", I("tensor_tensor", out=a_[:kp, c0:], in0=e_[:kp, c0:], in1=p_[:kp, c0:], op=ALU.mult),
             [be, bpp], [ba])
        P.op("dve", I("tensor_tensor", out=s_t[:kp, c0:], in0=s_t[:kp, c0:], in1=lk_[:kp, c0:], op=ALU.add),
             [blk, b("s_t")], [b("s_t")])
        av_fn(a_, ba)

    def sb_block(ebuf_i, zt, zb, c0, kparts, first, mask_fn, av_fn):
        sb_stage_a(ebuf_i, zt, zb, c0, kparts, mask_fn)
        sb_stage_b(ebuf_i, c0, kparts, first, av_fn)

    def layer(l, t, sample):
        NS, TP = (1, 64) if sample else (4, 128)
        NT = NS * TP
        tok0 = t * TILE
        bxs = [b(f"x{j}") for j in range(NS)]

        ld("sp", gvr[:], gv_rep[:, l * 512:(l + 1) * 512], "gvr")
        if sample:
            ld("sp", bsr_s[:], bs_rep_s[:, l * 256:(l + 1) * 256], "bsr_s")
        else:
            ld("sp", bsr[:], bs_rep[:, l * 512:(l + 1) * 512], "bsr")
        rms_to_hT(GOFF_LNMIX, l, NS, TP)
        hb = [b("hT")]
        wt, wb = wload(w_in, l, 0, 8, 0, 512)
        fm_proj(wt, wb, 8, [(g, g * 128) for g in range(4)], lambda kc: hT[:, kc, :NT], hb, NT,
                lambda fc, pt, pb: P.op("act", I("activation", out=uT[:, fc, :NT], in_=pt[:, :NT], func=AF.Gelu_apprx_tanh),
                                        [pb], [b("uT")]))
        wt, wb = wload(w_in, l, 0, 8, 512, 512)

        def evac_va(j, pt, pb):
            P.op("act", I("activation", out=vg[:TP, :], in_=pt[:TP, :], func=AF.Gelu_apprx_tanh), [pb], [b("vg")])
            P.op("dve", I("tensor_tensor", out=vsq[:TP, :], in0=vg[:TP, :], in1=vg[:TP, :], op=ALU.mult), [b("vg")], [b("vsq")])
            P.op("dve", I("tensor_reduce", out=small[:TP, 8:12], in_=vsq[:TP, :].rearrange("p (g e) -> p g e", g=4),
                                                  axis=AX.X, op=ALU.add), [b("vsq")], [b("small")])
            pow_rstd(small[:TP, 8:12], 128, TP)
            P.op("dve", I("tensor_tensor", out=vsq[:TP, :].rearrange("p (g e) -> p g e", g=4),
                                                  in0=vg[:TP, :].rearrange("p (g e) -> p g e", g=4),
                                                  in1=small[:TP, 8:12].unsqueeze(2).to_broadcast([TP, 4, 128]), op=ALU.mult),
                 [b("vg"), b("small")], [b("vsq")])
            if sample:
                st, sbf = stage()
                P.op("dve", I("tensor_tensor", out=st[:TP, :], in0=vsq[:TP, :], in1=gvr[:TP, :], op=ALU.mult),
                     [b("vsq"), b("gvr")], [sbf])
                P.op("pool", I("tensor_copy", out=vnA[:TP, j, :], in_=st[:TP, :]), [sbf], [b("vnA")])
                store(sguv_s[l, :, :], st[:TP, :], sbf, "sguv_s")
            else:
                P.op("dve", I("tensor_tensor", out=vnA[:TP, j, :], in0=vsq[:TP, :], in1=gvr[:TP, :], op=ALU.mult),
                     [b("vsq"), b("gvr")], [b("vnA")])
        tm_proj(wt, wb, 8, lambda kc, j: hT[:, kc, j * TP:(j + 1) * TP], hb, 512, NS, TP, evac_va)
        wt, wb = wload(w_in, l, 0, 8, 1024, 512)
        fm_proj(wt, wb, 8, [(hp, hp * 128) for hp in range(4)], lambda kc: hT[:, kc, :NT], hb, NT,
                lambda fc, pt, pb: P.op("act", I("activation", out=qk[:, fc, :NT], in_=pt[:, :NT], func=AF.Identity, scale=0.125),
                                        [pb], [b("qk")]))
        wt, wb = wload(w_in, l, 0, 8, 1536, 512)
        fm_proj(wt, wb, 8, [(hp, hp * 128) for hp in range(4)], lambda kc: hT[:, kc, :NT], hb, NT,
                lambda fc, pt, pb: P.op("dve", I("tensor_copy", out=qk[:, 4 + fc, :NT], in_=pt[:, :NT]), [pb], [b("qk")]))

        def evac_k(j, pt, pb):
            st, sbf = stage()
            P.op("act", I("activation", out=st[:TP, :], in_=pt[:TP, :], func=AF.Identity), [pb], [sbf])
            if sample:
                store(sbk_s[l, :, :], st[:TP, :], sbf, "sbk_s")
            else:
                store(sbk_p[l, tok0 + j * 128:tok0 + (j + 1) * 128, :], st[:TP, :], sbf, "sbk_p")
        tm_proj(wt, wb, 8, lambda kc, j: hT[:, kc, j * TP:(j + 1) * TP], hb, 512, NS, TP, evac_k)
        wt, wb = wload(w_in, l, 0, 8, 2048, 512)

        def evac_v(j, pt, pb):
            st, sbf = stage()
            P.op("act", I("activation", out=st[:TP, :], in_=pt[:TP, :], func=AF.Identity), [pb], [sbf])
            P.op("dve", I("tensor_copy", out=vn_b[:TP, j, :], in_=st[:TP, :]), [sbf], [b("vn_b")])
            if sample:
                store(sbv_s[l, :, :], st[:TP, :], sbf, "sbv_s")
            else:
                store(sbv_p[l, tok0 + j * 128:tok0 + (j + 1) * 128, :], st[:TP, :], sbf, "sbv_p")
        tm_proj(wt, wb, 8, lambda kc, j: hT[:, kc, j * TP:(j + 1) * TP], hb, 512, NS, TP, evac_v)
        if not sample and t < n_tiles - 1:
            for hp in range(4):
                P.dma("sp", I("dma_start", out=ktc_d[l, hp, :, tok0:tok0 + TILE], in_=qk[:, 4 + hp, :]),
                      b("qk"), b(f"ktc_d{l}"), b("qk"))
            P.dma("sp", I("dma_start", out=vc_d[l, tok0:tok0 + TILE, :].rearrange("(j p) n -> p j n", p=128), in_=vn_b[:]),
                  b("vn_b"), b(f"vc_d{l}"), b("vn_b"))

        for j in range(NS):
            pt, pb = mmbank()
            for g in range(4):
                if sample:
                    P.op("pe", I("matmul", pt[:, g * 64:(g + 1) * 64], lhsT=vnA[:64, 0, g * 128:(g + 1) * 128],
                                                              rhs=wsTs_sb[:64, l * 4 + g, :], start=(g == 0), stop=True, skip_group_check=True),
                         [b("vnA"), b("wsTs_sb")], [pb])
                else:
                    P.op("pe", I("matmul", pt[:, g * 128:(g + 1) * 128], lhsT=vnA[:, j, g * 128:(g + 1) * 128],
                                                                   rhs=wsT_sb[:, l * 4 + g, :], start=(g == 0), stop=True, skip_group_check=True),
                         [b("vnA"), b("wsT_sb")], [pb])
            if sample:
                P.op("dve", I("tensor_tensor", out=vg[:, 0:256], in0=pt[:, 0:256], in1=bsr_s[:, :], op=ALU.add),
                     [pb, b("bsr_s")], [b("vg")])
                P.op("dve", I("tensor_tensor", out=aT[:, :, 0:64], in0=vg[:, 0:256].rearrange("p (g t) -> p g t", g=4),
                                                      in1=uT[:, :, 0:64], op=ALU.mult), [b("vg"), b("uT")], [b("aT")])
            else:
                P.op("dve", I("tensor_tensor", out=vg[:, :], in0=pt[:, :], in1=bsr[:, :], op=ALU.add),
                     [pb, b("bsr")], [b("vg")])
                P.op("dve", I("tensor_tensor", out=aT[:, :, j * 128:(j + 1) * 128], in0=vg[:, :].rearrange("p (g t) -> p g t", g=4),
                                                           in1=uT[:, :, j * 128:(j + 1) * 128], op=ALU.mult), [b("vg"), b("uT")], [b("aT")])

        blk_i = [0]
        if not sample:
            for hp in range(4):
                s = hp % 2
                if t > 0:
                    P.dma("sp", I("dma_start", out=ktc[s][:, 0:tok0], in_=ktc_d[l, hp, :, 0:tok0]),
                          b(f"ktc_d{l}"), b(f"ktc{s}"), b(f"ktc{s}"))
                    P.dma("sp", I("dma_start",
                        out=vcs[s][:, 0:t * 4, :], in_=vc_d[l, 0:tok0, hp * 128:(hp + 1) * 128].rearrange("(k p) n -> p k n", p=128)),
                        b(f"vc_d{l}"), b(f"vcs{s}"), b(f"vcs{s}"))
                for i2 in range(2):
                    h = hp * 2 + i2
                    pr = slice(i2 * 64, i2 * 64 + 64)
                    P.op("dve", I("memset", s_t[:], 0.0), [], [b("s_t")])
                    blocks = [("d", jj) for jj in (3, 2, 1, 0)] + [("p", kb) for kb in range(t * 4 - 1, -1, -1)]
                    plan = []
                    for bi, (kind, idx) in enumerate(blocks):
                        i = blk_i[0] % 2
                        blk_i[0] += 1
                        if kind == "d":
                            c0 = idx * 128
                            klhs, kbufs = qk[pr, 4 + hp, idx * 128:(idx + 1) * 128], [b("qk")]
                            vsrc, vbuf = vn_b[:, idx, h * 64:(h + 1) * 64], b("vn_b")
                        else:
                            c0 = 0
                            klhs, kbufs = ktc[s][pr, idx * 128:(idx + 1) * 128], [b(f"ktc{s}"), b("qk")]
                            vsrc, vbuf = vcs[s][:, idx, i2 * 64:(i2 + 1) * 64], b(f"vcs{s}")
                        plan.append(dict(i=i, kind=kind, c0=c0, klhs=klhs, kbufs=kbufs, vsrc=vsrc, vbuf=vbuf,
                                         first=(bi == 0), last=(bi == len(blocks) - 1)))

                    def emit_a(d):
                        zt, zb = zp[d["i"]], b(f"zp{d['i']}")
                        c0 = d["c0"]
                        P.op("pe", I("matmul", zt[:, c0:], lhsT=d["klhs"], rhs=qk[pr, hp, c0:TILE], start=True, stop=True), d["kbufs"], [zb])
                        mask_fn = None
                        if d["kind"] == "d":
                            def mask_fn(e_, be, c0=c0):
                                P.op("dve", I("tensor_tensor", out=e_[:, c0:c0 + 128], in0=e_[:, c0:c0 + 128], in1=tri_m[:], op=ALU.mult),
                                     [b("tri_m"), be], [be])
                        sb_stage_a(d["i"], zt, zb, c0, 128, mask_fn)

                    def emit_b(d):
                        c0 = d["c0"]

                        def av_fn(a_, ba):
                            P.op("pe", I("matmul", op_[pr, c0:], lhsT=d["vsrc"], rhs=a_[:, c0:], start=d["first"], stop=d["last"],
                                         skip_group_check=True), [d["vbuf"], ba], [b("op_")])
                        sb_stage_b(d["i"], c0, 128, d["first"], av_fn)
                    emit_a(plan[0])
                    for n_ in range(len(plan)):
                        if n_ + 1 < len(plan):
                            emit_a(plan[n_ + 1])
                        emit_b(plan[n_])
                P.op("act", I("activation", out=sq4[:, hp, :], in_=op_[:, :], func=AF.Square), [b("op_")], [b("sq4")])
                P.op("act", I("activation", out=sbTb[:, hp, :], in_=op_[:, :], func=AF.Identity,
                                                          scale=gfm[:, GOFF_GB + l * 4 + hp:GOFF_GB + l * 4 + hp + 1]), [b("op_"), b("gfm")], [b("sbTb")])
        else:
            P.op("pool", I("memset", s_t[:], 0.0), [], [b("s_t")])
            i = blk_i[0] % 2
            blk_i[0] += 1
            zt, zb = zp[i], b(f"zp{i}")
            for i2 in range(2):
                pr = slice(i2 * 64, i2 * 64 + 64)
                for hp in range(4):
                    h = hp * 2 + i2
                    P.op("pe", I("matmul", zt[:64, h * 64:(h + 1) * 64], lhsT=qk[pr, 4 + hp, 0:64], rhs=qk[pr, hp, 0:64],
                                 start=(i2 == 0 and hp == 0), stop=True, skip_group_check=True), [b("qk")], [zb],
                         pe_sync=(i2 == 1 and hp == 0))

            def mask_new(e_, be):
                P.op("pool", I("tensor_tensor", out=e_[:64, :], in0=e_[:64, :], in1=msk_f[0:64, 320:832], op=ALU.mult),
                     [b("msk_f"), be], [be])

            def av_new(a_, ba):
                for h in range(8):
                    hp, i2 = h // 2, h % 2
                    P.op("pe", I("matmul", op_[i2 * 64:i2 * 64 + 64, hp * 64:(hp + 1) * 64],
                                                                     lhsT=vn_b[:64, 0, h * 64:(h + 1) * 64], rhs=a_[:64, h * 64:(h + 1) * 64],
                                                                     start=(h < 2), stop=False, skip_group_check=True), [b("vn_b"), ba], [b("op_")])
            sb_block(i, zt, zb, 0, 64, True, mask_new, av_new)
            for kb in range(7, -1, -1):
                s = kb % 2
                P.dma("pool", I("dma_start", out=kc_tok[s][:], in_=csk[l, :, kb * 128:(kb + 1) * 128, :].rearrange("s p n -> p s n")),
                      b("dram_in"), b(f"kc_tok{s}"), b(f"kc_tok{s}"))
                P.dma("pool", I("dma_start", out=vc_tok[s][:], in_=csv[l, :, kb * 128:(kb + 1) * 128, :].rearrange("s p n -> p s n")),
                      b("dram_in"), b(f"vc_tok{s}"), b(f"vc_tok{s}"))
                for st_ in range(NSTREAM):
                    tp_ap, tb = trbank()
                    for hp in range(4):
                        P.op("pe", I("transpose",
                            out=tp_ap[:, hp * 128:(hp + 1) * 128], in_=kc_tok[s][:, st_, hp * 128:(hp + 1) * 128], identity=ident[:]),
                            [b(f"kc_tok{s}"), b("ident")], [tb])
                    P.op("dve", I("tensor_copy", out=kcT[s][:, st_, :, :].rearrange("p a k -> p (a k)"), in_=tp_ap),
                         [tb], [b(f"kcT{s}")])
                i = blk_i[0] % 2
                blk_i[0] += 1
                zt, zb = zp[i], b(f"zp{i}")
                for i2 in range(2):
                    pr = slice(i2 * 64, i2 * 64 + 64)
                    for st_ in range(NSTREAM):
                        for hp in range(4):
                            h = hp * 2 + i2
                            P.op("pe", I("matmul", zt[:, h * 64 + st_ * 16:h * 64 + st_ * 16 + 16], lhsT=kcT[s][pr, st_, hp, :],
                                         rhs=qk[pr, hp, st_ * 16:st_ * 16 + 16], start=(i2 == 0 and st_ == 0 and hp == 0), stop=True,
                                         skip_group_check=True), [b(f"kcT{s}"), b("qk")], [zb],
                                 pe_sync=(i2 == 1 and st_ == 0 and hp == 0))

                def av_c(a_, ba, s=s, kb=kb):
                    for st_ in range(NSTREAM):
                        for h in range(8):
                            hp, i2 = h // 2, h % 2
                            P.op("pe", I("matmul",
                                op_[i2 * 64:i2 * 64 + 64, hp * 64 + st_ * 16:hp * 64 + st_ * 16 + 16],
                                lhsT=vc_tok[s][:, st_, h * 64:(h + 1) * 64], rhs=a_[:, h * 64 + st_ * 16:h * 64 + st_ * 16 + 16],
                                start=False, stop=(kb == 0), skip_group_check=True), [b(f"vc_tok{s}"), ba], [b("op_")])
                sb_block(i, zt, zb, 0, 128, False, None, av_c)
            P.op("act", I("activation", out=sq4[:, :, 0:64], in_=op_[:, 0:256].rearrange("p (a t) -> p a t", a=4), func=AF.Square),
                 [b("op_")], [b("sq4")])
            for hp in range(4):
                P.op("act", I("activation", out=sbTb[:, hp, 0:64], in_=op_[:, hp * 64:(hp + 1) * 64], func=AF.Identity,
                                                          scale=gfm[:, GOFF_GB + l * 4 + hp:GOFF_GB + l * 4 + hp + 1]), [b("op_"), b("gfm")], [b("sbTb")])

        def feat_rstd(col, square_from=None):
            if square_from is not None:
                srcT, srcb = square_from
                P.op("dve", I("tensor_tensor", out=sq4[:, :, :NT], in0=srcT[:, :, :NT], in1=srcT[:, :, :NT], op=ALU.mult), [srcb], [b("sq4")])
            pt, pb = mmbank()
            for j in range(NS):
                for c in range(4):
                    P.op("pe", I("matmul", pt[:TP, j:j + 1], lhsT=sq4[:, c, j * TP:(j + 1) * TP],
                                                                   rhs=ones_b[:, 0:1], start=(c == 0 and j == 0), stop=(c == 3), skip_group_check=True), [b("sq4"), b("ones_b")], [pb])
            P.op("dve", I("tensor_copy", out=small[:TP, 16 + col:16 + col + NS], in_=pt[:TP, 0:NS]), [pb], [b("small")])
            pow_rstd(small[:TP, 16 + col:16 + col + NS], 512, TP)
        feat_rstd(4)
        feat_rstd(0, (aT, b("aT")))
        for c in range(4):
            P.op("act", I("activation", out=aT[:, c, :NT], in_=aT[:, c, :NT], func=AF.Identity,
                          scale=gfm[:, GOFF_GA + l * 4 + c:GOFF_GA + l * 4 + c + 1]), [b("gfm"), b("aT")], [b("aT")])
        for half in range(2):
            wt, wb = wload(w_out, l, 0, 8, half * 512, 512)
            for (srcT, srcb, kc0, col) in ((aT, b("aT"), 0, 0), (sbTb, b("sbTb"), 4, 4)):
                def evac_m(j, pt, pb, col=col, half=half):
                    P.op("dve", I("scalar_tensor_tensor", out=x_t[:TP, j, half * 512:(half + 1) * 512], in0=pt[:TP, :],
                                                                 scalar=small[:TP, 16 + col + j:16 + col + j + 1],
                                                                 in1=x_t[:TP, j, half * 512:(half + 1) * 512], op0=ALU.mult, op1=ALU.add),
                         [pb, b("small"), bxs[j]], [bxs[j]])
                tm_proj(wt, wb, 4, lambda kc, j, srcT=srcT: srcT[:, kc, j * TP:(j + 1) * TP], [srcb], 512, NS, TP, evac_m, kc0=kc0)

        rms_to_hT(GOFF_LNMEM, l, NS, TP)
        for half in range(2):
            wt, wb = wload(w_mq, l, 0, 8, half * 512, 512)
            fm_proj(wt, wb, 8, [(half * 4 + c, c * 128) for c in range(4)], lambda kc: hT[:, kc, :NT], hb, NT,
                    lambda fc, pt, pb: P.op("act", I("activation", out=qk[:, fc, :NT], in_=pt[:, :NT], func=AF.Identity, scale=1.0 / 16),
                                            [pb], [b("qk")]))
        pti = [0]
        if not sample:
            P.dma("sp", I("dma_start", out=mkT[:], in_=mkT_d[l]), b(f"mkT_d{l}"), b("mkT"), b("mkT"))
            P.dma("sp", I("dma_start", out=mvb[:], in_=mvb_d[l]), b(f"mvb_d{l}"), b("mvb"), b("mvb"))
            for h in range(4):
                den_t, den_b = mm[0], b("mm0")
                o_ts = [(mm[1], b("mm1")), (bp[0], b("bp0"))]
                ptile = pT_t[pti[0] % 2]
                ptb = b(f"pT_t{pti[0] % 2}")
                pti[0] += 1
                for kb in range(2):
                    zt, zb = zp[kb], b(f"zp{kb}")
                    for dc in range(2):
                        P.op("pe", I("matmul",
                            zt[:, :], lhsT=mkT[:, 2 * h + dc, kb * 128:(kb + 1) * 128], rhs=qk[:, 2 * h + dc, :],
                            start=(dc == 0), stop=(dc == 1)), [b("mkT"), b("qk")], [zb])
                    P.op("act", I("activation", out=ptile[:, kb, :], in_=zt[:, :], func=AF.Exp), [zb], [ptb])
                for kb in range(2):
                    P.op("pe", I("matmul", den_t[:, :], lhsT=ones_b[:], rhs=ptile[:, kb, :],
                                                                                  start=(kb == 0), stop=(kb == 1)), [b("ones_b"), ptb], [den_b])
                for dc in range(2):
                    ot, ob = o_ts[dc]
                    for kb in range(2):
                        P.op("pe", I("matmul",
                            ot[:, :], lhsT=mvb[:, kb, h * 256 + dc * 128:h * 256 + (dc + 1) * 128], rhs=ptile[:, kb, :],
                            start=(kb == 0), stop=(kb == 1)), [b("mvb"), ptb], [ob])
                P.op("dve", I("reciprocal", out=rden[:, :], in_=den_t[:, :]), [den_b], [b("rden")])
                for dc in range(2):
                    ot, ob = o_ts[dc]
                    P.op("dve", I("tensor_tensor", out=hT[:, 2 * h + dc, :], in0=ot[:, :], in1=rden[:, :], op=ALU.mult),
                         [ob, b("rden")], [b("hT")])
        else:
            den_t, den_b = bp[1], b("bp1")
            ot, ob = bp[0], b("bp0")
            for st_ in range(NSTREAM):
                if st_ % 2 == 0:
                    mk_s, mk_b, mv_s, mv_b = mkT, b("mkT"), mvb, b("mvb")
                else:
                    mk_s, mk_b, mv_s, mv_b = smk1, b("smk1"), smv1, b("smv1")
                P.dma("pool", I("dma_start", out=mk_tok[:], in_=cmk[l, st_].rearrange("(k p) n -> p k n", p=128)),
                      b("dram_in"), b("mk_tok"), b("mk_tok"))
                P.dma("pool", I("dma_start", out=mv_s[:], in_=cmv[l, st_].rearrange("(k p) n -> p k n", p=128)),
                      b("dram_in"), mv_b, mv_b)
                for kb in range(2):
                    for half in range(2):
                        tp_ap, tb = trbank()
                        for cc in range(4):
                            c = half * 4 + cc
                            P.op("pe", I("transpose", out=tp_ap[:, cc * 128:(cc + 1) * 128],
                                                                                            in_=mk_tok[:, kb, c * 128:(c + 1) * 128], identity=ident[:]),
                                 [b("mk_tok"), b("ident")], [tb])
                        P.op("dve", I("tensor_copy",
                            out=mk_s[:, half * 4:half * 4 + 4, kb * 128:(kb + 1) * 128], in_=tp_ap.rearrange("p (c k) -> p c k", c=4)),
                            [tb], [mk_b])
                q0 = st_ * 16
                for h in range(4):
                    ptile = pT_t[pti[0] % 2]
                    ptb = b(f"pT_t{pti[0] % 2}")
                    pti[0] += 1
                    for kb in range(2):
                        zt, zb = zp[kb], b(f"zp{kb}")
                        for dc in range(2):
                            P.op("pe", I("matmul",
                                zt[:, 0:16], lhsT=mk_s[:, 2 * h + dc, kb * 128:(kb + 1) * 128], rhs=qk[:, 2 * h + dc, q0:q0 + 16],
                                start=(dc == 0), stop=(dc == 1)), [mk_b, b("qk")], [zb])
                        P.op("act", I("activation", out=ptile[:, kb, 0:16], in_=zt[:, 0:16], func=AF.Exp), [zb], [ptb])
                    for kb in range(2):
                        P.op("pe", I("matmul", den_t[:, h * 64 + q0:h * 64 + q0 + 16], lhsT=ones_b[:], rhs=ptile[:, kb, 0:16],
                                     start=(kb == 0 and h == 0 and st_ == 0), stop=(kb == 1), skip_group_check=True), [b("ones_b"), ptb], [den_b])
                    for dc in range(2):
                        for kb in range(2):
                            P.op("pe", I("matmul",
                                ot[:, dc * 256 + h * 64 + q0:dc * 256 + h * 64 + q0 + 16], lhsT=mv_s[:, kb, h * 256 + dc * 128:h * 256 + (dc + 1) * 128],
                                rhs=ptile[:, kb, 0:16], start=(kb == 0 and dc == 0 and h == 0 and st_ == 0), stop=(kb == 1), skip_group_check=True),
                                [mv_b, ptb], [ob])
            P.op("dve", I("reciprocal", out=rden[:, 0:256], in_=den_t[:, 0:256]), [den_b], [b("rden")])
            for h in range(4):
                for dc in range(2):
                    P.op("dve", I("tensor_tensor", out=hT[:, 2 * h + dc, 0:64], in0=ot[:, dc * 256 + h * 64:dc * 256 + (h + 1) * 64],
                                                                      in1=rden[:, h * 64:(h + 1) * 64], op=ALU.mult), [ob, b("rden")], [b("hT")])
        for half in range(2):
            wt, wb = wload(w_mo, l, 0, 8, half * 512, 512)

            def evac_o(j, pt, pb, half=half):
                P.op("dve", I("tensor_tensor", out=x_t[:TP, j, half * 512:(half + 1) * 512], in0=pt[:TP, :],
                                                      in1=x_t[:TP, j, half * 512:(half + 1) * 512], op=ALU.add), [pb, bxs[j]], [bxs[j]])
            tm_proj(wt, wb, 8, lambda kc, j: hT[:, kc, j * TP:(j + 1) * TP], hb, 512, NS, TP, evac_o)

        rms_to_hT(GOFF_LNFFN, l, NS, TP)
        for (kb0, nkc) in ((0, 8), (8, 8), (16, 6)):
            for c0 in range(0, nkc * 128, 512):
                ncol = min(512, nkc * 128 - c0)
                wtg, wbg = wload(w_g, l, 0, 8, kb0 * 128 + c0, ncol)
                wtu, wbu = wload(w_u, l, 0, 8, kb0 * 128 + c0, ncol)
                for c in range(ncol // 128):
                    fc = c0 // 128 + c
                    pg, pgb = mmbank()
                    for kc in range(8):
                        P.op("pe", I("matmul", pg[:, :NT], lhsT=wtg[:, kc, c * 128:(c + 1) * 128], rhs=hT[:, kc, :NT],
                                                                                 start=(kc == 0), stop=(kc == 7)), [wbg, b("hT")], [pgb])
                    P.op("act", I("activation", out=vg[:, :NT], in_=pg[:, :NT], func=AF.Silu), [pgb], [b("vg")])
                    pu, pub = mmbank()
                    for kc in range(8):
                        P.op("pe", I("matmul", pu[:, :NT], lhsT=wtu[:, kc, c * 128:(c + 1) * 128], rhs=hT[:, kc, :NT],
                                                                                 start=(kc == 0), stop=(kc == 7)), [wbu, b("hT")], [pub])
                    P.op("dve", I("tensor_tensor", out=actT[:, fc, :NT], in0=pu[:, :NT], in1=vg[:, :NT], op=ALU.mult),
                         [pub, b("vg")], [b("actT")])
            for half in range(2):
                wt, wb = wload(w_d, l, kb0 * 128, nkc, half * 512, 512)

                def evac_d(j, pt, pb, half=half):
                    P.op("dve", I("tensor_tensor", out=x_t[:TP, j, half * 512:(half + 1) * 512], in0=pt[:TP, :],
                                                          in1=x_t[:TP, j, half * 512:(half + 1) * 512], op=ALU.add), [pb, bxs[j]], [bxs[j]])
                tm_proj(wt, wb, nkc, lambda kc, j: actT[:, kc, j * TP:(j + 1) * TP], [b("actT")], 512, NS, TP, evac_d)

    def final_norm(t, sample):
        NS, TP = (1, 64) if sample else (4, 128)
        ld("sp", gvr[:], lnf_rep[:, 0:512], "gvr")
        ld("sp", bsr[:], lnf_rep[:, 512:1024], "bsr")
        lnh = [(gvr, b("gvr")), (bsr, b("bsr"))]
        for j in range(NS):
            bx = b(f"x{j}")
            P.op("act", I("activation", out=sqs[:TP, :], in_=x_t[:TP, j, :], func=AF.Square), [bx], [b("sqs")])
            P.op("dve", I("tensor_reduce", out=small[:TP, j:j + 1], in_=sqs[:TP, :], axis=AX.X, op=ALU.add), [b("sqs")], [b("small")])
            pow_rstd(small[:TP, j:j + 1], D, TP)
            for half in range(2):
                st, sbf = stage()
                P.op("dve", I("scalar_tensor_tensor",
                    out=st[:TP, :], in0=x_t[:TP, j, half * 512:(half + 1) * 512], scalar=small[:TP, j:j + 1],
                    in1=lnh[half][0][:TP, :], op0=ALU.mult, op1=ALU.mult), [bx, b("small"), lnh[half][1]], [sbf])
                if sample:
                    store(y_s[:, half * 512:(half + 1) * 512], st[:TP, :], sbf, "y_s")
                else:
                    store(y_p[t * TILE + j * 128:t * TILE + (j + 1) * 128, half * 512:(half + 1) * 512], st[:TP, :], sbf, "y_p")

    def mem_prologue():
        for kb in range(2):
            P.dma("pool", I("dma_start", out=mk_tok[:, kb, :], in_=memp[kb * 128:(kb + 1) * 128, :]),
                  b("dram_in"), b("mk_tok"), b("mk_tok"))
        for kb in range(2):
            for half in range(2):
                tp_ap, tb = trbank()
                for cc in range(4):
                    c = half * 4 + cc
                    P.op("pe", I("transpose", out=tp_ap[:, cc * 128:(cc + 1) * 128],
                                                                                    in_=mk_tok[:, kb, c * 128:(c + 1) * 128], identity=ident[:]),
                         [b("mk_tok"), b("ident")], [tb])
                P.op("dve", I("tensor_copy",
                    out=memT[:, half * 4:half * 4 + 4, kb * 128:(kb + 1) * 128], in_=tp_ap.rearrange("p (c k) -> p c k", c=4)), [tb], [b("memT")])
        for l in range(depth):
            for (wsrc, outd, is_k) in ((w_mk, mk_p, True), (w_mv, mv_p, False)):
                for half in range(2):
                    wt, wb = wload(wsrc, l, 0, 8, half * 512, 512)
                    for kb in range(2):
                        pt, pb = mmbank()
                        for kc in range(8):
                            P.op("pe", I("matmul", pt[:, :], lhsT=memT[:, kc, kb * 128:(kb + 1) * 128], rhs=wt[:, kc, :],
                                                                                     start=(kc == 0), stop=(kc == 7)), [wb, b("memT")], [pb])
                        st, sbf = stage()
                        P.op("act", I("activation", out=st[:, :], in_=pt[:, :], func=AF.Identity), [pb], [sbf])
                        store(outd[l, kb * 128:(kb + 1) * 128, half * 512:(half + 1) * 512], st[:, :], sbf, "mkv_out")
                        if not is_k:
                            P.op("dve", I("tensor_copy", out=mvb[:, kb, half * 512:(half + 1) * 512], in_=st[:, :]),
                                 [sbf], [b("mvb")])
                    if is_k:
                        for c in range(4):
                            fc = half * 4 + c
                            pt, pb = mmbank()
                            for kc in range(8):
                                P.op("pe", I("matmul", pt[:, :NMEM], lhsT=wt[:, kc, c * 128:(c + 1) * 128], rhs=memT[:, kc, :],
                                                                                       start=(kc == 0), stop=(kc == 7)), [wb, b("memT")], [pb])
                            P.op("dve", I("tensor_copy", out=mkT[:, fc, :], in_=pt[:, :NMEM]), [pb], [b("mkT")])
            P.dma("sp", I("dma_start", out=mkT_d[l], in_=mkT[:]), b("mkT"), b(f"mkT_d{l}"), b("mkT"))
            P.dma("sp", I("dma_start", out=mvb_d[l], in_=mvb[:]), b("mvb"), b(f"mvb_d{l}"), b("mvb"))

    mem_prologue()
    for t in range(n_tiles):
        for j in range(4):
            P.dma("sp", I("dma_start", out=x_t[:, j, :], in_=xp[t * TILE + j * 128:t * TILE + (j + 1) * 128, :]),
                  b("dram_in"), b(f"x{j}"), b(f"x{j}"))
        for l in range(depth):
            layer(l, t, False)
        final_norm(t, False)
    if with_sample:
        P.dma("sp", I("dma_start", out=x_t[:64, 0, :], in_=xs[:, :]), b("dram_in"), b("x0"), b("x0"))
        for l in range(depth):
            layer(l, 0, True)
        final_norm(0, True)
    P.finish()

    sems = {}
    for k in P.semkeys:
        sems[k] = es.enter_context(nc.semaphore("s_" + "_".join(str(v) for v in k)))
    engmap = {"pe": "tensor", "act": "scalar", "dve": "vector", "pool": "gpsimd", "sp": "sync"}
    with es:
        with nc.Block() as block:
            def make(engname):
                def body(eng):
                    for (waits, fn, inc) in P.q[engname]:
                        for (k, v) in waits:
                            eng.wait_ge(sems[k], v)
                        if fn is not None:
                            ins = fn(eng)
                            ins.then_inc(sems[inc[0]], inc[1])
                return body
            for en in ENGS:
                getattr(block, engmap[en])(make(en))
    counts = {e: len(P.q[e]) for e in ENGS}
    return nc, counts


def _host_consts(inp):
    L = DEPTH
    f = np.float32

    def fm(v, nch):
        return np.ascontiguousarray(np.asarray(v, f).reshape(nch, 128).T)
    cols = []
    for name in ("ln_mix", "ln_mem", "ln_ffn"):
        for l in range(L):
            cols.append(fm(inp[name][l], 8))
    cols.append(fm(inp["ln_final"], 8))
    for name in ("g_out_sgu", "g_out_sb"):
        for l in range(L):
            cols.append(fm(inp[name][l], 4))
    gains_fm = np.ascontiguousarray(np.concatenate(cols, axis=1))
    gv = np.asarray(inp["g_sgu_v"], f).reshape(1, L * 512)
    gv_rep = np.ascontiguousarray(np.broadcast_to(gv, (128, L * 512)))
    bs = np.asarray(inp["b_sgu"], f)
    bs_rep = np.ascontiguousarray(np.broadcast_to(bs.reshape(1, L * 512), (128, L * 512)))
    bs_s = np.tile(bs[:, :, :16], (1, 1, 4))
    bs_rep_s = np.ascontiguousarray(np.broadcast_to(bs_s.reshape(1, L * 256), (128, L * 256)))
    ws = np.asarray(inp["w_sgu"], f)
    wsT = np.ascontiguousarray(ws.transpose(0, 1, 3, 2))
    wsT_s = np.zeros((L, 4, 64, 64), f)
    for s in range(4):
        wsT_s[:, :, s * 16:(s + 1) * 16, s * 16:(s + 1) * 16] = wsT[:, :, :16, :16]
    k = np.arange(128)
    tri = (k[:, None] < k[None, :]).astype(f)
    negtri = -(k[:, None] >= k[None, :]).astype(f)
    k64 = np.arange(64)
    same = (k64[:, None] // 16) == (k64[None, :] // 16)
    bd = (same & (k64[:, None] <= k64[None, :])).astype(f)
    newm = (same & (k64[:, None] < k64[None, :])).astype(f)
    newmask = np.zeros((128, 512), f)
    newmask[:64, :] = np.tile(newm, (1, 8))
    m = np.zeros((128, 960), f)
    m[:, 832:960] = np.eye(128, dtype=f)
    m[:, 0:128] = tri
    m[:, 128:256] = negtri
    m[:64, 256:320] = bd
    m[:, 320:832] = newmask
    lnf_rep = np.ascontiguousarray(np.broadcast_to(np.asarray(inp["ln_final"], f).reshape(1, D), (128, D)))
    return dict(gains_fm=gains_fm, gv_rep=gv_rep, bs_rep=bs_rep, bs_rep_s=bs_rep_s, wsT=wsT, wsT_s=wsT_s, masks=m, lnf_rep=lnf_rep)


_CACHE = {}


def kernel(**inputs):
    inp = {k: np.asarray(v) for k, v in inputs.items()}
    n_tiles = int(inp.pop("_n_tiles", SEQ // TILE)) if "_n_tiles" in inp else SEQ // TILE
    depth = int(inp.pop("_depth", DEPTH)) if "_depth" in inp else DEPTH
    key = (n_tiles, depth)
    if key not in _CACHE:
        _CACHE[key] = build(n_tiles, depth, os.environ.get("KNOSAMPLE") is None)
    nc, counts = _CACHE[key]
    consts = _host_consts(inp)
    f = np.float32
    shared = {
        "w_in": inp["w_in"], "w_out": inp["w_out"], "w_mq": inp["w_mq"], "w_mk": inp["w_mk"], "w_mv": inp["w_mv"],
        "w_mo": inp["w_mo"], "w_g": inp["w_ffn_gate"], "w_u": inp["w_ffn_up"], "w_d": inp["w_ffn_down"],
    }
    shared = {k: np.ascontiguousarray(v, dtype=f) for k, v in shared.items()}
    shared.update(consts)
    in_maps = []
    for c in range(8):
        bq = c // 2
        s0 = c * NSTREAM
        m = dict(shared)
        m["xp"] = np.ascontiguousarray(inp["x_prompt"][bq], dtype=f)
        m["xs"] = np.ascontiguousarray(inp["x_sample"][s0:s0 + NSTREAM].reshape(64, D), dtype=f)
        m["csk"] = np.ascontiguousarray(inp["cache_sb_k"][:, s0:s0 + NSTREAM].reshape(DEPTH, NSTREAM, PAST, 512), dtype=f)
        m["csv"] = np.ascontiguousarray(inp["cache_sb_v"][:, s0:s0 + NSTREAM].reshape(DEPTH, NSTREAM, PAST, 512), dtype=f)
        m["cmk"] = np.ascontiguousarray(inp["cache_mem_k"][:, s0:s0 + NSTREAM].reshape(DEPTH, NSTREAM, NMEM, D), dtype=f)
        m["cmv"] = np.ascontiguousarray(inp["cache_mem_v"][:, s0:s0 + NSTREAM].reshape(DEPTH, NSTREAM, NMEM, D), dtype=f)
        m["memp"] = np.ascontiguousarray(inp["mem_prompt"][bq], dtype=f)
        in_maps.append(m)
    ncores = int(os.environ.get("KCORES", "8"))
    res = run_bass_kernel_spmd(nc, in_maps[:ncores], core_ids=list(range(ncores)))
    R = list(res.results) + [res.results[0]] * (8 - ncores)
    B4 = 4

    def g(c, name):
        return np.asarray(R[c][name], dtype=f)
    y_prompt = np.stack([g(2 * bq, "y_p") for bq in range(B4)])
    y_sample = np.concatenate([g(c, "y_s").reshape(NSTREAM, DSEQ, D) for c in range(8)], axis=0)
    sb_k_prompt = np.stack([g(2 * bq, "sbk_p") for bq in range(B4)], axis=1).reshape(DEPTH, B4, SEQ, 8, 64)
    sb_v_prompt = np.stack([g(2 * bq, "sbv_p") for bq in range(B4)], axis=1).reshape(DEPTH, B4, SEQ, 8, 64)
    mem_k_prompt = np.stack([g(2 * bq, "mk_p") for bq in range(B4)], axis=1).reshape(DEPTH, B4, NMEM, 4, 256)
    mem_v_prompt = np.stack([g(2 * bq, "mv_p") for bq in range(B4)], axis=1).reshape(DEPTH, B4, NMEM, 4, 256)
    sb_k_sample = np.concatenate([g(c, "sbk_s").reshape(DEPTH, NSTREAM, DSEQ, 8, 64) for c in range(8)], axis=1)
    sb_v_sample = np.concatenate([g(c, "sbv_s").reshape(DEPTH, NSTREAM, DSEQ, 8, 64) for c in range(8)], axis=1)
    sgu_v_sample = np.concatenate([g(c, "sguv_s").reshape(DEPTH, NSTREAM, DSEQ, 4, 128) for c in range(8)], axis=1)
    return (y_prompt, y_sample, sb_k_prompt, sb_v_prompt, mem_k_prompt, mem_v_prompt, sb_k_sample, sb_v_sample, sgu_v_sample)
```

```python
import contextlib
import os
import numpy as np
import concourse.bass as bass
import concourse.mybir as mybir
from concourse.bass_utils import run_bass_kernel_spmd

F32 = mybir.dt.float32
BF16 = mybir.dt.bfloat16
AF = mybir.ActivationFunctionType
ALU = mybir.AluOpType
AX = mybir.AxisListType

D = 1024
SEQ = 4096
DEPTH = 4
NSTREAM = 4
DSEQ = 16
PAST = 1024
NMEM = 256
DFF = 2816
NFC = DFF // 128
EPS = 1e-6
TILE = 512

ENGS = ("pe", "act", "dve", "pool", "sp")


def I(meth, *a, **k):
    f = lambda e: getattr(e, meth)(*a, **k)
    f.desc = (meth, a, k)
    return f


PSUM_BUFS = {"mm0", "mm1", "zp0", "zp1", "bp0", "bp1", "op_", "trp"}


class Buf:
    __slots__ = ("name", "lw", "rd", "dma_tot")

    def __init__(self, name):
        self.name = name
        self.lw = None
        self.rd = {}
        self.dma_tot = 0


class Prog:
    CH = 12000

    def __init__(self):
        self.q = {e: [] for e in ENGS}
        self.n = {e: 0 for e in ENGS}
        self.waited = {e: {} for e in ENGS}
        self.semkeys = {}
        self.out_events = {}
        self.nrec = 0
        self.kstop = int(os.environ.get("KSTOP", "1000000000"))

    def _wait_list(self, eng, deps, pe_sync=False):
        waits = []
        for (key, val, src) in deps:
            if src == eng and eng == "pe" and not pe_sync:
                continue
            if self.waited[eng].get(key, 0) >= val:
                continue
            self.waited[eng][key] = val
            waits.append((key, val))
        return waits

    def op(self, eng, fn, reads=(), writes=(), pe_sync=False):
        self.nrec += 1
        if self.nrec > self.kstop:
            return
        deps = []
        for b in reads:
            if b.lw is not None:
                deps.append(b.lw)
            if b.name in PSUM_BUFS:
                deps.extend(ev for ev in b.rd.values() if ev[2] != eng)
        for b in writes:
            if b.lw is not None:
                deps.append(b.lw)
            deps.extend(b.rd.values())
        i = self.n[eng]
        self.n[eng] += 1
        key = ("e", eng, i // self.CH)
        ev = (key, i % self.CH + 1, eng)
        self.semkeys[key] = True
        waits = self._wait_list(eng, deps, pe_sync)
        self.q[eng].append((waits, fn, (key, 1)))
        for b in writes:
            b.lw = ev
            b.rd = {}
        for b in reads:
            if b not in writes:
                b.rd[key] = ev

    def dma(self, eng, fn, src, dst, side, is_output=False):
        self.nrec += 1
        if self.nrec > self.kstop:
            return
        deps = []
        if src.lw is not None:
            deps.append(src.lw)
        if dst.lw is not None:
            deps.append(dst.lw)
        deps.extend(dst.rd.values())
        key = ("d", side.name, eng)
        self.semkeys[key] = True
        if not isinstance(side.dma_tot, dict):
            side.dma_tot = {}
        tot = side.dma_tot.get(key, 0)
        if tot > 0:
            deps.append((key, tot, "dma"))
        tot += 16
        side.dma_tot[key] = tot
        ev = (key, tot, "dma")
        waits = self._wait_list(eng, deps)
        self.q[eng].append((waits, fn, (key, 16)))
        dst.lw = ev
        dst.rd = {}
        src.rd[key] = ev
        if is_output:
            self.out_events[key] = ev

    def finish(self):
        waits = self._wait_list("pool", list(self.out_events.values()))
        self.q["pool"].append((waits, None, None))


def build(n_tiles=SEQ // TILE, depth=DEPTH, with_sample=True):
    nc = bass.Bass("TRN2", target_bir_lowering=False)
    P = Prog()
    L = DEPTH

    def din(name, shape, dt=F32):
        return nc.dram_tensor(name, list(shape), dt, kind="ExternalInput").ap()

    def dout(name, shape, dt=F32):
        return nc.dram_tensor(name, list(shape), dt, kind="ExternalOutput").ap()

    xp = din("xp", [SEQ, D])
    xs = din("xs", [64, D])
    csk = din("csk", [L, NSTREAM, PAST, 512])
    csv = din("csv", [L, NSTREAM, PAST, 512])
    cmk = din("cmk", [L, NSTREAM, NMEM, D])
    cmv = din("cmv", [L, NSTREAM, NMEM, D])
    memp = din("memp", [NMEM, D])
    w_in = din("w_in", [L, D, 2560])
    w_out = din("w_out", [L, D, D])
    w_mq = din("w_mq", [L, D, D])
    w_mk = din("w_mk", [L, D, D])
    w_mv = din("w_mv", [L, D, D])
    w_mo = din("w_mo", [L, D, D])
    w_g = din("w_g", [L, D, DFF])
    w_u = din("w_u", [L, D, DFF])
    w_d = din("w_d", [L, DFF, D])
    gains_fm = din("gains_fm", [128, 3 * L * 8 + 8 + 2 * L * 4])
    gv_rep = din("gv_rep", [128, L * 512])
    bs_rep = din("bs_rep", [128, L * 512])
    bs_rep_s = din("bs_rep_s", [128, L * 256])
    wsT = din("wsT", [L, 4, 128, 128])
    wsT_s = din("wsT_s", [L, 4, 64, 64])
    masks = din("masks", [128, 960])
    lnf_rep = din("lnf_rep", [128, D])

    y_p = dout("y_p", [SEQ, D])
    y_s = dout("y_s", [64, D])
    sbk_p = dout("sbk_p", [L, SEQ, 512])
    sbv_p = dout("sbv_p", [L, SEQ, 512])
    mk_p = dout("mk_p", [L, NMEM, D])
    mv_p = dout("mv_p", [L, NMEM, D])
    sbk_s = dout("sbk_s", [L, 64, 512])
    sbv_s = dout("sbv_s", [L, 64, 512])
    sguv_s = dout("sguv_s", [L, 64, 512])

    WSRC = {"w_in": w_in, "w_out": w_out, "w_mq": w_mq, "w_mo": w_mo, "w_g": w_g, "w_u": w_u, "w_d": w_d}
    WBF = {k: nc.dram_tensor(k + "_bf", list(v.shape), BF16).ap() for k, v in WSRC.items()}
    WNAME = {id(v): k for k, v in WSRC.items()}
    ktc_d = nc.dram_tensor("ktc_d", [L, 4, 128, SEQ], BF16).ap()
    vc_d = nc.dram_tensor("vc_d", [L, SEQ, 512], BF16).ap()
    mkT_d = nc.dram_tensor("mkT_d", [L, 128, 8, NMEM], BF16).ap()
    mvb_d = nc.dram_tensor("mvb_d", [L, 128, 2, D], BF16).ap()

    es = contextlib.ExitStack()

    def sb(name, shape, dt):
        return es.enter_context(nc.sbuf_tensor(name, list(shape), dt))

    def ps(name, shape, dt=F32):
        return es.enter_context(nc.psum_tensor(name, list(shape), dt))

    x_t = sb("x_t", [128, 4, D], F32)
    hT = sb("hT", [128, 8, TILE], BF16)
    xn = sb("xn", [128, D], BF16)
    sqs = sb("sqs", [128, D], F32)
    NW = 3
    wr = [sb(f"wr{i}", [128, 8, 512], BF16) for i in range(NW)]
    qk = sb("qk", [128, 8, TILE], BF16)
    vn_b = sb("vn_b", [128, 4, 512], BF16)
    stg = [sb(f"stg{i}", [128, 512], F32) for i in range(3)]
    ktc = [sb(f"ktc{i}", [128, SEQ], BF16) for i in range(2)]
    vcs = [sb(f"vcs{i}", [128, (SEQ - TILE) // 128, 128], BF16) for i in range(2)]
    e_t = [sb(f"e_t{i}", [128, 512], F32) for i in range(3)]
    lk_t = [sb(f"lk_t{i}", [128, 512], BF16) for i in range(3)]
    p_t = [sb(f"p_t{i}", [128, 512], BF16) for i in range(2)]
    a_t = [sb(f"a_t{i}", [128, 512], BF16) for i in range(2)]
    s_t = sb("s_t", [128, 512], BF16)
    uT = sb("uT", [128, 4, TILE], BF16)
    vg = sb("vg", [128, 512], F32)
    vsq = sb("vsq", [128, 512], F32)
    vnA = sb("vnA", [128, 4, 512], BF16)
    aT = sb("aT", [128, 4, TILE], BF16)
    sbTb = sb("sbTb", [128, 4, TILE], BF16)
    sq4 = sb("sq4", [128, 4, TILE], BF16)
    actT = sb("actT", [128, 8, TILE], BF16)
    mkT = sb("mkT", [128, 8, NMEM], BF16)
    mvb = sb("mvb", [128, 2, D], BF16)
    pT_t = [sb(f"pT_t{i}", [128, 2, TILE], BF16) for i in range(2)]
    rden = sb("rden", [128, TILE], F32)
    small = sb("small", [128, 64], F32)
    gfm = sb("gfm", [128, 3 * L * 8 + 8 + 2 * L * 4], F32)
    gvr = sb("gvr", [128, 512], F32)
    bsr = sb("bsr", [128, 512], F32)
    bsr_s = sb("bsr_s", [128, 256], F32)
    wsT_sb = sb("wsT_sb", [128, L * 4, 128], BF16)
    wsTs_sb = sb("wsTs_sb", [64, L * 4, 64], BF16)
    msk_f = sb("msk_f", [128, 960], F32)
    tri_m = sb("tri_m", [128, 128], F32)
    negtri = sb("negtri", [128, 128], BF16)
    negones = sb("negones", [128, 128], BF16)
    ones_b = sb("ones_b", [128, 128], BF16)
    ident = sb("ident", [128, 128], BF16)
    bd_m = sb("bd_m", [64, 64], BF16)
    kc_tok = [ktc[i][:, 0:2048].rearrange("p (s n) -> p s n", s=4) for i in range(2)]
    kcT = [ktc[i][:, 2048:4096].rearrange("p (s a k) -> p s a k", s=4, a=4) for i in range(2)]
    vc_tok = [vcs[i][:, 0:16, :].rearrange("p (s a) k -> p s (a k)", s=4) for i in range(2)]
    mk_tok = sb("mk_tok", [128, 2, D], BF16)
    memT = actT[:, 0:4, :].rearrange("p a (b k) -> p (a b) k", b=2)
    smk1 = sb("smk1", [128, 8, NMEM], BF16)
    smv1 = sb("smv1", [128, 2, D], BF16)

    mm = [ps(f"mm{i}", [128, 512]) for i in range(2)]
    zp = [ps(f"zp{i}", [128, 512]) for i in range(2)]
    bp = [ps(f"bp{i}", [128, 512]) for i in range(2)]
    op_ = ps("op_", [128, 512])
    trp = ps("trp", [128, 1024], BF16)

    B = {}

    ALIAS = {"kc_tok0": "ktc0", "kc_tok1": "ktc1", "kcT0": "ktc0", "kcT1": "ktc1", "vc_tok0": "vcs0", "vc_tok1": "vcs1",
             "memT": "actT"}

    def b(name):
        name = ALIAS.get(name, name)
        if name not in B:
            B[name] = Buf(name)
        return B[name]

    GOFF_LNMIX, GOFF_LNMEM, GOFF_LNFFN = 0, L * 8, 2 * L * 8
    GOFF_LNF = 3 * L * 8
    GOFF_GA = GOFF_LNF + 8
    GOFF_GB = GOFF_GA + L * 4

    def ld(eng, dst_ap, src_ap, dname, sname="dram_in"):
        P.dma(eng, I("dma_start", out=dst_ap, in_=src_ap), b(sname), b(dname), b(dname))

    ld("sp", gfm[:], gains_fm[:, :], "gfm")
    ld("sp", msk_f[:], masks[:, :], "msk_f")
    ld("pool", wsT_sb[:], wsT.rearrange("l g s t -> s (l g) t"), "wsT_sb")
    ld("pool", wsTs_sb[:], wsT_s.rearrange("l g s t -> s (l g) t"), "wsTs_sb")
    P.op("dve", I("tensor_copy", out=tri_m[:], in_=msk_f[:, 0:128]), [b("msk_f")], [b("tri_m")])
    P.op("dve", I("tensor_copy", out=negtri[:], in_=msk_f[:, 128:256]), [b("msk_f")], [b("negtri")])
    P.op("dve", I("tensor_copy", out=bd_m[:], in_=msk_f[0:64, 256:320]), [b("msk_f")], [b("bd_m")])
    epsb = sb("epsb", [128, 1], F32)
    P.op("pool", I("memset", epsb[:], EPS), [], [b("epsb")])
    P.op("pool", I("memset", negones[:], -1.0), [], [b("negones")])
    P.op("pool", I("memset", ones_b[:], 1.0), [], [b("ones_b")])
    P.op("dve", I("tensor_copy", out=ident[:], in_=msk_f[:, 832:960]), [b("msk_f")], [b("ident")])
    incl = sb("incl", [128, 128], BF16)
    P.op("dve", I("tensor_tensor", out=incl[:], in0=tri_m[:], in1=ident[:], op=ALU.add),
         [b("tri_m"), b("ident")], [b("incl")])
    for i in range(L * 4):
        P.op("pool", I("tensor_tensor", out=wsT_sb[:, i, :], in0=wsT_sb[:, i, :], in1=incl[:], op=ALU.mult),
             [b("incl"), b("wsT_sb")], [b("wsT_sb")])
        P.op("pool", I("tensor_tensor", out=wsTs_sb[:, i, :], in0=wsTs_sb[:, i, :], in1=bd_m[:], op=ALU.mult),
             [b("bd_m"), b("wsTs_sb")], [b("wsTs_sb")])

    wstate = {"i": 0}

    def wload(w_ap, l, r0, nkc, c0, ncols):
        s = wstate["i"] % NW
        wstate["i"] += 1
        t = wr[s]
        nm = WNAME.get(id(w_ap))
        if nm is not None:
            src = WBF[nm][l, r0:r0 + nkc * 128, c0:c0 + ncols].rearrange("(kc p) n -> p kc n", p=128)
            P.dma("sp", I("dma_start", out=t[:, 0:nkc, 0:ncols], in_=src), b(f"wbf{l}_{nm}"), b(f"wr{s}"), b(f"wr{s}"))
        else:
            src = w_ap[l, r0:r0 + nkc * 128, c0:c0 + ncols].rearrange("(kc p) n -> p kc n", p=128)
            P.dma("pool", I("dma_start", out=t[:, 0:nkc, 0:ncols], in_=src), b("dram_in"), b(f"wr{s}"), b(f"wr{s}"))
        return t, b(f"wr{s}")

    def cast_layer(l):
        for nm in ("w_in", "w_out", "w_mq", "w_mo", "w_g", "w_u", "w_d"):
            rows = WSRC[nm].shape[1]
            for r0 in range(0, rows, 128):
                P.dma("pool", I("dma_start", out=WBF[nm][l, r0:r0 + 128, :], in_=WSRC[nm][l, r0:r0 + 128, :]),
                      b("dram_in"), b(f"wbf{l}_{nm}"), b(f"wcast{l}"))

    mmstate = {"i": 0}

    MMROT = [(mm[0], "mm0"), (mm[1], "mm1"), (zp[0], "zp0"), (zp[1], "zp1"), (bp[1], "bp1")]

    def mmbank():
        t_, n_ = MMROT[mmstate["i"] % len(MMROT)]
        mmstate["i"] += 1
        return t_, b(n_)

    stgstate = {"i": 0}

    def stage():
        i = stgstate["i"] % 3
        stgstate["i"] += 1
        return stg[i], b(f"stg{i}")

    trstate = {"i": 0}

    mm1_bf = mm[1][:].bitcast(BF16)

    def trbank():
        i = trstate["i"] % 2
        trstate["i"] += 1
        return trp[:, i * 512:(i + 1) * 512], b("trp")

    def store(dst_ap, src_ap, sbuf_buf, out_name):
        P.dma("pool", I("dma_start", out=dst_ap, in_=src_ap), sbuf_buf, b(out_name), sbuf_buf, is_output=True)

    def rms_to_hT(goff, l, NS, TP):
        NT = NS * TP
        for j in range(NS):
            bx = b(f"x{j}")
            P.op("act", I("activation", out=sqs[:TP, :], in_=x_t[:TP, j, :], func=AF.Square),
                 [bx], [b("sqs")])
            P.op("dve", I("tensor_reduce", out=small[:TP, j:j + 1], in_=sqs[:TP, :], axis=AX.X, op=ALU.add),
                 [b("sqs")], [b("small")])
            pow_rstd(small[:TP, j:j + 1], D, TP)
            P.op("dve", I("tensor_scalar", out=xn[:TP, :], in0=x_t[:TP, j, :], scalar1=small[:TP, j:j + 1],
                                                       scalar2=None, op0=ALU.mult), [bx, b("small")], [b("xn")])
            for half in range(2):
                tp_ap, tb = trbank()
                for cc in range(4):
                    c = half * 4 + cc
                    P.op("pe", I("transpose",
                        out=tp_ap[:, cc * 128:cc * 128 + TP], in_=xn[:TP, c * 128:(c + 1) * 128], identity=ident[:TP, :TP]),
                        [b("xn"), b("ident")], [tb])
                for cc in range(4):
                    c = half * 4 + cc
                    P.op("act", I("activation",
                        out=hT[:, c, j * TP:(j + 1) * TP], in_=tp_ap[:, cc * 128:cc * 128 + TP], func=AF.Identity,
                        scale=gfm[:, goff + l * 8 + c:goff + l * 8 + c + 1]), [tb, b("gfm")], [b("hT")])

    def fm_proj(wt, wb, nkc, fcs, rhs_fn, rhs_bufs, NT, evac):
        for (fc, col0) in fcs:
            pt, pb = mmbank()
            for kc in range(nkc):
                P.op("pe", I("matmul",
                    pt[:, :NT], lhsT=wt[:, kc, col0:col0 + 128], rhs=rhs_fn(kc), start=(kc == 0), stop=(kc == nkc - 1)),
                    [wb] + rhs_bufs, [pb])
            evac(fc, pt, pb)

    def tm_proj(wt, wb, nkc, lhs_fn, lhs_bufs, ncols, NS, TP, evac, kc0=0):
        for j in range(NS):
            pt, pb = mmbank()
            for kc in range(nkc):
                P.op("pe", I("matmul",
                    pt[:TP, :ncols], lhsT=lhs_fn(kc, j), rhs=wt[:, kc0 + kc, 0:ncols], start=(kc == 0), stop=(kc == nkc - 1)),
                    [wb] + lhs_bufs, [pb])
            evac(j, pt, pb)

    def pow_rstd(ap, n, TP):
        P.op("act", I("activation", out=ap, in_=ap, func=AF.Ln, scale=1.0 / n, bias=epsb[:TP, 0:1]), [b("small"), b("epsb")], [b("small")])
        P.op("act", I("activation", out=ap, in_=ap, func=AF.Exp, scale=-0.5), [b("small")], [b("small")])

    def sb_stage_a(i, zt, zb, c0, kp, mask_fn):
        e_, lk_ = e_t[i], lk_t[i]
        be, blk = b(f"e{i}"), b(f"lk{i}")
        P.op("act", I("activation", out=e_[:kp, c0:], in_=zt[:kp, c0:], func=AF.Exp), [zb], [be])
        if mask_fn is not None:
            mask_fn(e_, be)
        P.op("act", I("activation", out=lk_[:kp, c0:], in_=e_[:kp, c0:], func=AF.Ln, bias=1.0, scale=1.0), [be], [blk])

    def sb_stage_b(i, c0, kp, first, av_fn, ib=None):
        ib = i if ib is None else ib
        e_, lk_, p_, a_ = e_t[i], lk_t[i], p_t[ib], a_t[ib]
        be, blk, bpp, ba = b(f"e{i}"), b(f"lk{i}"), b(f"p{ib}"), b(f"a{ib}")
        bt, bb = bp[ib], b(f"bp{ib}")
        P.op("pe", I("matmul", bt[:kp, c0:], lhsT=negtri[:kp, :kp], rhs=lk_[:kp, c0:], start=True, stop=first),
             [b("negtri"), blk], [bb])
        if not first:
            P.op("pe", I("matmul", bt[:kp, c0:], lhsT=negones[:, :kp], rhs=s_t[:, c0:], start=False, stop=True),
                 [b("negones"), b("s_t")], [bb])
        P.op("act", I("activation", out=p_[:kp, c0:], in_=bt[:kp, c0:], func=AF.Exp), [bb], [bpp])
        P.op("dve", I("tensor_tensor", out=a_[:kp, c0:], in0=e_[:kp, c0:], in1=p_[:kp, c0:], op=ALU.mult),
             [be, bpp], [ba])
        P.op("dve", I("tensor_tensor", out=s_t[:kp, c0:], in0=s_t[:kp, c0:], in1=lk_[:kp, c0:], op=ALU.add),
             [blk, b("s_t")], [b("s_t")])
        av_fn(a_, ba)

    def sb_block(ebuf_i, zt, zb, c0, kparts, first, mask_fn, av_fn):
        sb_stage_a(ebuf_i, zt, zb, c0, kparts, mask_fn)
        sb_stage_b(ebuf_i, c0, kparts, first, av_fn)

    def layer(l, t, sample):
        NS, TP = (1, 64) if sample else (4, 128)
        NT = NS * TP
        tok0 = t * TILE
        bxs = [b(f"x{j}") for j in range(NS)]

        ld("sp", gvr[:], gv_rep[:, l * 512:(l + 1) * 512], "gvr")
        if sample:
            ld("sp", bsr_s[:], bs_rep_s[:, l * 256:(l + 1) * 256], "bsr_s")
        else:
            ld("sp", bsr[:], bs_rep[:, l * 512:(l + 1) * 512], "bsr")
        rms_to_hT(GOFF_LNMIX, l, NS, TP)
        hb = [b("hT")]
        wt, wb = wload(w_in, l, 0, 8, 0, 512)
        fm_proj(wt, wb, 8, [(g, g * 128) for g in range(4)], lambda kc: hT[:, kc, :NT], hb, NT,
                lambda fc, pt, pb: P.op("act", I("activation", out=uT[:, fc, :NT], in_=pt[:, :NT], func=AF.Gelu_apprx_tanh),
                                        [pb], [b("uT")]))
        wt, wb = wload(w_in, l, 0, 8, 512, 512)

        def evac_va(j, pt, pb):
            P.op("act", I("activation", out=vg[:TP, :], in_=pt[:TP, :], func=AF.Gelu_apprx_tanh), [pb], [b("vg")])
            P.op("dve", I("tensor_tensor", out=vsq[:TP, :], in0=vg[:TP, :], in1=vg[:TP, :], op=ALU.mult), [b("vg")], [b("vsq")])
            P.op("dve", I("tensor_reduce", out=small[:TP, 8:12], in_=vsq[:TP, :].rearrange("p (g e) -> p g e", g=4),
                                                  axis=AX.X, op=ALU.add), [b("vsq")], [b("small")])
            pow_rstd(small[:TP, 8:12], 128, TP)
            P.op("dve", I("tensor_tensor", out=vsq[:TP, :].rearrange("p (g e) -> p g e", g=4),
                                                  in0=vg[:TP, :].rearrange("p (g e) -> p g e", g=4),
                                                  in1=small[:TP, 8:12].unsqueeze(2).to_broadcast([TP, 4, 128]), op=ALU.mult),
                 [b("vg"), b("small")], [b("vsq")])
            if sample:
                st, sbf = stage()
                P.op("dve", I("tensor_tensor", out=st[:TP, :], in0=vsq[:TP, :], in1=gvr[:TP, :], op=ALU.mult),
                     [b("vsq"), b("gvr")], [sbf])
                P.op("pool", I("tensor_copy", out=vnA[:TP, j, :], in_=st[:TP, :]), [sbf], [b("vnA")])
                store(sguv_s[l, :, :], st[:TP, :], sbf, "sguv_s")
            else:
                P.op("dve", I("tensor_tensor", out=vnA[:TP, j, :], in0=vsq[:TP, :], in1=gvr[:TP, :], op=ALU.mult),
                     [b("vsq"), b("gvr")], [b("vnA")])
        tm_proj(wt, wb, 8, lambda kc, j: hT[:, kc, j * TP:(j + 1) * TP], hb, 512, NS, TP, evac_va)
        wt, wb = wload(w_in, l, 0, 8, 1024, 512)
        fm_proj(wt, wb, 8, [(hp, hp * 128) for hp in range(4)], lambda kc: hT[:, kc, :NT], hb, NT,
                lambda fc, pt, pb: P.op("act", I("activation", out=qk[:, fc, :NT], in_=pt[:, :NT], func=AF.Identity, scale=0.125),
                                        [pb], [b("qk")]))
        wt, wb = wload(w_in, l, 0, 8, 1536, 512)
        fm_proj(wt, wb, 8, [(hp, hp * 128) for hp in range(4)], lambda kc: hT[:, kc, :NT], hb, NT,
                lambda fc, pt, pb: P.op("dve", I("tensor_copy", out=qk[:, 4 + fc, :NT], in_=pt[:, :NT]), [pb], [b("qk")]))

        def evac_k(j, pt, pb):
            st, sbf = stage()
            P.op("act", I("activation", out=st[:TP, :], in_=pt[:TP, :], func=AF.Identity), [pb], [sbf])
            if sample:
                store(sbk_s[l, :, :], st[:TP, :], sbf, "sbk_s")
            else:
                store(sbk_p[l, tok0 + j * 128:tok0 + (j + 1) * 128, :], st[:TP, :], sbf, "sbk_p")
        tm_proj(wt, wb, 8, lambda kc, j: hT[:, kc, j * TP:(j + 1) * TP], hb, 512, NS, TP, evac_k)
        wt, wb = wload(w_in, l, 0, 8, 2048, 512)

        def evac_v(j, pt, pb):
            st, sbf = stage()
            P.op("act", I("activation", out=st[:TP, :], in_=pt[:TP, :], func=AF.Identity), [pb], [sbf])
            P.op("dve", I("tensor_copy", out=vn_b[:TP, j, :], in_=st[:TP, :]), [sbf], [b("vn_b")])
            if sample:
                store(sbv_s[l, :, :], st[:TP, :], sbf, "sbv_s")
            else:
                store(sbv_p[l, tok0 + j * 128:tok0 + (j + 1) * 128, :], st[:TP, :], sbf, "sbv_p")
        tm_proj(wt, wb, 8, lambda kc, j: hT[:, kc, j * TP:(j + 1) * TP], hb, 512, NS, TP, evac_v)
        if not sample and t < n_tiles - 1:
            for hp in range(4):
                P.dma("pool", I("dma_start", out=ktc_d[l, hp, :, tok0:tok0 + TILE], in_=qk[:, 4 + hp, :]),
                      b("qk"), b(f"ktc_d{l}"), b("qk"))
            P.dma("pool", I("dma_start", out=vc_d[l, tok0:tok0 + TILE, :].rearrange("(j p) n -> p j n", p=128), in_=vn_b[:]),
                  b("vn_b"), b(f"vc_d{l}"), b("vn_b"))

        for j in range(NS):
            pt, pb = mmbank()
            for g in range(4):
                if sample:
                    P.op("pe", I("matmul", pt[:, g * 64:(g + 1) * 64], lhsT=vnA[:64, 0, g * 128:(g + 1) * 128],
                                                              rhs=wsTs_sb[:64, l * 4 + g, :], start=(g == 0), stop=True, skip_group_check=True),
                         [b("vnA"), b("wsTs_sb")], [pb])
                else:
                    P.op("pe", I("matmul", pt[:, g * 128:(g + 1) * 128], lhsT=vnA[:, j, g * 128:(g + 1) * 128],
                                                                   rhs=wsT_sb[:, l * 4 + g, :], start=(g == 0), stop=True, skip_group_check=True),
                         [b("vnA"), b("wsT_sb")], [pb])
            if sample:
                P.op("dve", I("tensor_tensor", out=vg[:, 0:256], in0=pt[:, 0:256], in1=bsr_s[:, :], op=ALU.add),
                     [pb, b("bsr_s")], [b("vg")])
                P.op("dve", I("tensor_tensor", out=aT[:, :, 0:64], in0=vg[:, 0:256].rearrange("p (g t) -> p g t", g=4),
                                                      in1=uT[:, :, 0:64], op=ALU.mult), [b("vg"), b("uT")], [b("aT")])
            else:
                P.op("dve", I("tensor_tensor", out=vg[:, :], in0=pt[:, :], in1=bsr[:, :], op=ALU.add),
                     [pb, b("bsr")], [b("vg")])
                P.op("dve", I("tensor_tensor", out=aT[:, :, j * 128:(j + 1) * 128], in0=vg[:, :].rearrange("p (g t) -> p g t", g=4),
                                                           in1=uT[:, :, j * 128:(j + 1) * 128], op=ALU.mult), [b("vg"), b("uT")], [b("aT")])

        blk_i = [0]
        blk3 = [0]
        if not sample:
            for hp in range(4):
                s = hp % 2
                if t > 0:
                    P.dma("sp", I("dma_start", out=ktc[s][:, 0:tok0], in_=ktc_d[l, hp, :, 0:tok0]),
                          b(f"ktc_d{l}"), b(f"ktc{s}"), b(f"ktc{s}"))
                    P.dma("sp", I("dma_start",
                        out=vcs[s][:, 0:t * 4, :], in_=vc_d[l, 0:tok0, hp * 128:(hp + 1) * 128].rearrange("(k p) n -> p k n", p=128)),
                        b(f"vc_d{l}"), b(f"vcs{s}"), b(f"vcs{s}"))
                for i2 in range(2):
                    h = hp * 2 + i2
                    pr = slice(i2 * 64, i2 * 64 + 64)
                    P.op("dve", I("memset", s_t[:], 0.0), [], [b("s_t")])
                    blocks = [("d", jj) for jj in (3, 2, 1, 0)] + [("p", kb) for kb in range(t * 4 - 1, -1, -1)]
                    plan = []
                    for bi, (kind, idx) in enumerate(blocks):
                        i = blk_i[0] % 2
                        blk_i[0] += 1
                        if kind == "d":
                            c0 = idx * 128
                            klhs, kbufs = qk[pr, 4 + hp, idx * 128:(idx + 1) * 128], [b("qk")]
                            vsrc, vbuf = vn_b[:, idx, h * 64:(h + 1) * 64], b("vn_b")
                        else:
                            c0 = 0
                            klhs, kbufs = ktc[s][pr, idx * 128:(idx + 1) * 128], [b(f"ktc{s}"), b("qk")]
                            vsrc, vbuf = vcs[s][:, idx, i2 * 64:(i2 + 1) * 64], b(f"vcs{s}")
                        plan.append(dict(i=i, ia=blk3[0] % 3, kind=kind, c0=c0, klhs=klhs, kbufs=kbufs, vsrc=vsrc, vbuf=vbuf,
                                         first=(bi == 0), last=(bi == len(blocks) - 1)))
                        blk3[0] += 1

                    def emit_qk(d):
                        zt, zb = zp[d["i"]], b(f"zp{d['i']}")
                        c0 = d["c0"]
                        P.op("pe", I("matmul", zt[:, c0:], lhsT=d["klhs"], rhs=qk[pr, hp, c0:TILE], start=True, stop=True), d["kbufs"], [zb])

                    def emit_a(d):
                        zt, zb = zp[d["i"]], b(f"zp{d['i']}")
                        c0 = d["c0"]
                        mask_fn = None
                        if d["kind"] == "d":
                            def mask_fn(e_, be, c0=c0):
                                P.op("dve", I("tensor_tensor", out=e_[:, c0:c0 + 128], in0=e_[:, c0:c0 + 128], in1=tri_m[:], op=ALU.mult),
                                     [b("tri_m"), be], [be])
                        sb_stage_a(d["ia"], zt, zb, c0, 128, mask_fn)

                    def emit_b(d):
                        c0 = d["c0"]

                        def av_fn(a_, ba):
                            P.op("pe", I("matmul", op_[pr, c0:], lhsT=d["vsrc"], rhs=a_[:, c0:], start=d["first"], stop=d["last"],
                                         skip_group_check=True), [d["vbuf"], ba], [b("op_")])
                        sb_stage_b(d["ia"], c0, 128, d["first"], av_fn, ib=d["i"])
                    NB = len(plan)
                    emit_qk(plan[0])
                    if NB > 1:
                        emit_qk(plan[1])
                    emit_a(plan[0])
                    for n_ in range(NB):
                        if n_ + 2 < NB:
                            emit_qk(plan[n_ + 2])
                        if n_ + 1 < NB:
                            emit_a(plan[n_ + 1])
                        emit_b(plan[n_])
                P.op("act", I("activation", out=sq4[:, hp, :], in_=op_[:, :], func=AF.Square), [b("op_")], [b("sq4")])
                P.op("act", I("activation", out=sbTb[:, hp, :], in_=op_[:, :], func=AF.Identity,
                                                          scale=gfm[:, GOFF_GB + l * 4 + hp:GOFF_GB + l * 4 + hp + 1]), [b("op_"), b("gfm")], [b("sbTb")])
        else:
            P.op("pool", I("memset", s_t[:], 0.0), [], [b("s_t")])
            i = blk_i[0] % 2
            blk_i[0] += 1
            zt, zb = zp[i], b(f"zp{i}")
            for i2 in range(2):
                pr = slice(i2 * 64, i2 * 64 + 64)
                for hp in range(4):
                    h = hp * 2 + i2
                    P.op("pe", I("matmul", zt[:64, h * 64:(h + 1) * 64], lhsT=qk[pr, 4 + hp, 0:64], rhs=qk[pr, hp, 0:64],
                                 start=(i2 == 0 and hp == 0), stop=True, skip_group_check=True), [b("qk")], [zb],
                         pe_sync=(i2 == 1 and hp == 0))

            def mask_new(e_, be):
                P.op("pool", I("tensor_tensor", out=e_[:64, :], in0=e_[:64, :], in1=msk_f[0:64, 320:832], op=ALU.mult),
                     [b("msk_f"), be], [be])

            def av_new(a_, ba):
                for h in range(8):
                    hp, i2 = h // 2, h % 2
                    P.op("pe", I("matmul", op_[i2 * 64:i2 * 64 + 64, hp * 64:(hp + 1) * 64],
                                                                     lhsT=vn_b[:64, 0, h * 64:(h + 1) * 64], rhs=a_[:64, h * 64:(h + 1) * 64],
                                                                     start=(h < 2), stop=False, skip_group_check=True), [b("vn_b"), ba], [b("op_")])
            sb_block(i, zt, zb, 0, 64, True, mask_new, av_new)
            for kb in range(7, -1, -1):
                s = kb % 2
                P.dma("pool", I("dma_start", out=kc_tok[s][:], in_=csk[l, :, kb * 128:(kb + 1) * 128, :].rearrange("s p n -> p s n")),
                      b("dram_in"), b(f"kc_tok{s}"), b(f"kc_tok{s}"))
                P.dma("pool", I("dma_start", out=vc_tok[s][:], in_=csv[l, :, kb * 128:(kb + 1) * 128, :].rearrange("s p n -> p s n")),
                      b("dram_in"), b(f"vc_tok{s}"), b(f"vc_tok{s}"))
                for st_ in range(NSTREAM):
                    tp_ap, tb = trbank()
                    for hp in range(4):
                        P.op("pe", I("transpose",
                            out=tp_ap[:, hp * 128:(hp + 1) * 128], in_=kc_tok[s][:, st_, hp * 128:(hp + 1) * 128], identity=ident[:]),
                            [b(f"kc_tok{s}"), b("ident")], [tb])
                    P.op("dve", I("tensor_copy", out=kcT[s][:, st_, :, :].rearrange("p a k -> p (a k)"), in_=tp_ap),
                         [tb], [b(f"kcT{s}")])
                i = blk_i[0] % 2
                blk_i[0] += 1
                zt, zb = zp[i], b(f"zp{i}")
                for i2 in range(2):
                    pr = slice(i2 * 64, i2 * 64 + 64)
                    for st_ in range(NSTREAM):
                        for hp in range(4):
                            h = hp * 2 + i2
                            P.op("pe", I("matmul", zt[:, h * 64 + st_ * 16:h * 64 + st_ * 16 + 16], lhsT=kcT[s][pr, st_, hp, :],
                                         rhs=qk[pr, hp, st_ * 16:st_ * 16 + 16], start=(i2 == 0 and st_ == 0 and hp == 0), stop=True,
                                         skip_group_check=True), [b(f"kcT{s}"), b("qk")], [zb],
                                 pe_sync=(i2 == 1 and st_ == 0 and hp == 0))

                def av_c(a_, ba, s=s, kb=kb):
                    for st_ in range(NSTREAM):
                        for h in range(8):
                            hp, i2 = h // 2, h % 2
                            P.op("pe", I("matmul",
                                op_[i2 * 64:i2 * 64 + 64, hp * 64 + st_ * 16:hp * 64 + st_ * 16 + 16],
                                lhsT=vc_tok[s][:, st_, h * 64:(h + 1) * 64], rhs=a_[:, h * 64 + st_ * 16:h * 64 + st_ * 16 + 16],
                                start=False, stop=(kb == 0), skip_group_check=True), [b(f"vc_tok{s}"), ba], [b("op_")])
                sb_block(i, zt, zb, 0, 128, False, None, av_c)
            P.op("act", I("activation", out=sq4[:, :, 0:64], in_=op_[:, 0:256].rearrange("p (a t) -> p a t", a=4), func=AF.Square),
                 [b("op_")], [b("sq4")])
            for hp in range(4):
                P.op("act", I("activation", out=sbTb[:, hp, 0:64], in_=op_[:, hp * 64:(hp + 1) * 64], func=AF.Identity,
                                                          scale=gfm[:, GOFF_GB + l * 4 + hp:GOFF_GB + l * 4 + hp + 1]), [b("op_"), b("gfm")], [b("sbTb")])

        def feat_rstd(col, square_from=None):
            if square_from is not None:
                srcT, srcb = square_from
                P.op("dve", I("tensor_tensor", out=sq4[:, :, :NT], in0=srcT[:, :, :NT], in1=srcT[:, :, :NT], op=ALU.mult), [srcb], [b("sq4")])
            pt, pb = mmbank()
            for j in range(NS):
                for c in range(4):
                    P.op("pe", I("matmul", pt[:TP, j:j + 1], lhsT=sq4[:, c, j * TP:(j + 1) * TP],
                                                                   rhs=ones_b[:, 0:1], start=(c == 0 and j == 0), stop=(c == 3), skip_group_check=True), [b("sq4"), b("ones_b")], [pb])
            P.op("dve", I("tensor_copy", out=small[:TP, 16 + col:16 + col + NS], in_=pt[:TP, 0:NS]), [pb], [b("small")])
            pow_rstd(small[:TP, 16 + col:16 + col + NS], 512, TP)
        feat_rstd(4)
        feat_rstd(0, (aT, b("aT")))
        for c in range(4):
            P.op("act", I("activation", out=aT[:, c, :NT], in_=aT[:, c, :NT], func=AF.Identity,
                          scale=gfm[:, GOFF_GA + l * 4 + c:GOFF_GA + l * 4 + c + 1]), [b("gfm"), b("aT")], [b("aT")])
        for half in range(2):
            wt, wb = wload(w_out, l, 0, 8, half * 512, 512)
            for (srcT, srcb, kc0, col) in ((aT, b("aT"), 0, 0), (sbTb, b("sbTb"), 4, 4)):
                def evac_m(j, pt, pb, col=col, half=half):
                    P.op("dve", I("scalar_tensor_tensor", out=x_t[:TP, j, half * 512:(half + 1) * 512], in0=pt[:TP, :],
                                                                 scalar=small[:TP, 16 + col + j:16 + col + j + 1],
                                                                 in1=x_t[:TP, j, half * 512:(half + 1) * 512], op0=ALU.mult, op1=ALU.add),
                         [pb, b("small"), bxs[j]], [bxs[j]])
                tm_proj(wt, wb, 4, lambda kc, j, srcT=srcT: srcT[:, kc, j * TP:(j + 1) * TP], [srcb], 512, NS, TP, evac_m, kc0=kc0)

        rms_to_hT(GOFF_LNMEM, l, NS, TP)
        for half in range(2):
            wt, wb = wload(w_mq, l, 0, 8, half * 512, 512)
            fm_proj(wt, wb, 8, [(half * 4 + c, c * 128) for c in range(4)], lambda kc: hT[:, kc, :NT], hb, NT,
                    lambda fc, pt, pb: P.op("act", I("activation", out=qk[:, fc, :NT], in_=pt[:, :NT], func=AF.Identity, scale=1.0 / 16),
                                            [pb], [b("qk")]))
        pti = [0]
        if not sample:
            P.dma("sp", I("dma_start", out=mkT[:], in_=mkT_d[l]), b(f"mkT_d{l}"), b("mkT"), b("mkT"))
            P.dma("sp", I("dma_start", out=mvb[:], in_=mvb_d[l]), b(f"mvb_d{l}"), b("mvb"), b("mvb"))
            for h in range(4):
                den_t, den_b = mm[0], b("mm0")
                o_ts = [(mm[1], b("mm1")), (bp[0], b("bp0"))]
                ptile = pT_t[pti[0] % 2]
                ptb = b(f"pT_t{pti[0] % 2}")
                pti[0] += 1
                for kb in range(2):
                    zt, zb = zp[kb], b(f"zp{kb}")
                    for dc in range(2):
                        P.op("pe", I("matmul",
                            zt[:, :], lhsT=mkT[:, 2 * h + dc, kb * 128:(kb + 1) * 128], rhs=qk[:, 2 * h + dc, :],
                            start=(dc == 0), stop=(dc == 1)), [b("mkT"), b("qk")], [zb])
                    P.op("act", I("activation", out=ptile[:, kb, :], in_=zt[:, :], func=AF.Exp), [zb], [ptb])
                for kb in range(2):
                    P.op("pe", I("matmul", den_t[:, :], lhsT=ones_b[:], rhs=ptile[:, kb, :],
                                                                                  start=(kb == 0), stop=(kb == 1)), [b("ones_b"), ptb], [den_b])
                for dc in range(2):
                    ot, ob = o_ts[dc]
                    for kb in range(2):
                        P.op("pe", I("matmul",
                            ot[:, :], lhsT=mvb[:, kb, h * 256 + dc * 128:h * 256 + (dc + 1) * 128], rhs=ptile[:, kb, :],
                            start=(kb == 0), stop=(kb == 1)), [b("mvb"), ptb], [ob])
                P.op("dve", I("reciprocal", out=rden[:, :], in_=den_t[:, :]), [den_b], [b("rden")])
                for dc in range(2):
                    ot, ob = o_ts[dc]
                    P.op("dve", I("tensor_tensor", out=hT[:, 2 * h + dc, :], in0=ot[:, :], in1=rden[:, :], op=ALU.mult),
                         [ob, b("rden")], [b("hT")])
        else:
            den_t, den_b = bp[1], b("bp1")
            ot, ob = bp[0], b("bp0")
            for st_ in range(NSTREAM):
                if st_ % 2 == 0:
                    mk_s, mk_b, mv_s, mv_b = mkT, b("mkT"), mvb, b("mvb")
                else:
                    mk_s, mk_b, mv_s, mv_b = smk1, b("smk1"), smv1, b("smv1")
                P.dma("pool", I("dma_start", out=mk_tok[:], in_=cmk[l, st_].rearrange("(k p) n -> p k n", p=128)),
                      b("dram_in"), b("mk_tok"), b("mk_tok"))
                P.dma("pool", I("dma_start", out=mv_s[:], in_=cmv[l, st_].rearrange("(k p) n -> p k n", p=128)),
                      b("dram_in"), mv_b, mv_b)
                for kb in range(2):
                    for half in range(2):
                        tp_ap, tb = trbank()
                        for cc in range(4):
                            c = half * 4 + cc
                            P.op("pe", I("transpose", out=tp_ap[:, cc * 128:(cc + 1) * 128],
                                                                                            in_=mk_tok[:, kb, c * 128:(c + 1) * 128], identity=ident[:]),
                                 [b("mk_tok"), b("ident")], [tb])
                        P.op("dve", I("tensor_copy",
                            out=mk_s[:, half * 4:half * 4 + 4, kb * 128:(kb + 1) * 128], in_=tp_ap.rearrange("p (c k) -> p c k", c=4)),
                            [tb], [mk_b])
                q0 = st_ * 16
                for h in range(4):
                    ptile = pT_t[pti[0] % 2]
                    ptb = b(f"pT_t{pti[0] % 2}")
                    pti[0] += 1
                    for kb in range(2):
                        zt, zb = zp[kb], b(f"zp{kb}")
                        for dc in range(2):
                            P.op("pe", I("matmul",
                                zt[:, 0:16], lhsT=mk_s[:, 2 * h + dc, kb * 128:(kb + 1) * 128], rhs=qk[:, 2 * h + dc, q0:q0 + 16],
                                start=(dc == 0), stop=(dc == 1)), [mk_b, b("qk")], [zb])
                        P.op("act", I("activation", out=ptile[:, kb, 0:16], in_=zt[:, 0:16], func=AF.Exp), [zb], [ptb])
                    for kb in range(2):
                        P.op("pe", I("matmul", den_t[:, h * 64 + q0:h * 64 + q0 + 16], lhsT=ones_b[:], rhs=ptile[:, kb, 0:16],
                                     start=(kb == 0 and h == 0 and st_ == 0), stop=(kb == 1), skip_group_check=True), [b("ones_b"), ptb], [den_b])
                    for dc in range(2):
                        for kb in range(2):
                            P.op("pe", I("matmul",
                                ot[:, dc * 256 + h * 64 + q0:dc * 256 + h * 64 + q0 + 16], lhsT=mv_s[:, kb, h * 256 + dc * 128:h * 256 + (dc + 1) * 128],
                                rhs=ptile[:, kb, 0:16], start=(kb == 0 and dc == 0 and h == 0 and st_ == 0), stop=(kb == 1), skip_group_check=True),
                                [mv_b, ptb], [ob])
            P.op("dve", I("reciprocal", out=rden[:, 0:256], in_=den_t[:, 0:256]), [den_b], [b("rden")])
            for h in range(4):
                for dc in range(2):
                    P.op("dve", I("tensor_tensor", out=hT[:, 2 * h + dc, 0:64], in0=ot[:, dc * 256 + h * 64:dc * 256 + (h + 1) * 64],
                                                                      in1=rden[:, h * 64:(h + 1) * 64], op=ALU.mult), [ob, b("rden")], [b("hT")])
        for half in range(2):
            wt, wb = wload(w_mo, l, 0, 8, half * 512, 512)

            def evac_o(j, pt, pb, half=half):
                P.op("dve", I("tensor_tensor", out=x_t[:TP, j, half * 512:(half + 1) * 512], in0=pt[:TP, :],
                                                      in1=x_t[:TP, j, half * 512:(half + 1) * 512], op=ALU.add), [pb, bxs[j]], [bxs[j]])
            tm_proj(wt, wb, 8, lambda kc, j: hT[:, kc, j * TP:(j + 1) * TP], hb, 512, NS, TP, evac_o)

        rms_to_hT(GOFF_LNFFN, l, NS, TP)
        for (kb0, nkc) in ((0, 8), (8, 8), (16, 6)):
            for c0 in range(0, nkc * 128, 512):
                ncol = min(512, nkc * 128 - c0)
                wtg, wbg = wload(w_g, l, 0, 8, kb0 * 128 + c0, ncol)
                wtu, wbu = wload(w_u, l, 0, 8, kb0 * 128 + c0, ncol)
                for c in range(ncol // 128):
                    fc = c0 // 128 + c
                    pg, pgb = mmbank()
                    for kc in range(8):
                        P.op("pe", I("matmul", pg[:, :NT], lhsT=wtg[:, kc, c * 128:(c + 1) * 128], rhs=hT[:, kc, :NT],
                                                                                 start=(kc == 0), stop=(kc == 7)), [wbg, b("hT")], [pgb])
                    P.op("act", I("activation", out=vg[:, :NT], in_=pg[:, :NT], func=AF.Silu), [pgb], [b("vg")])
                    pu, pub = mmbank()
                    for kc in range(8):
                        P.op("pe", I("matmul", pu[:, :NT], lhsT=wtu[:, kc, c * 128:(c + 1) * 128], rhs=hT[:, kc, :NT],
                                                                                 start=(kc == 0), stop=(kc == 7)), [wbu, b("hT")], [pub])
                    P.op("dve", I("tensor_tensor", out=actT[:, fc, :NT], in0=pu[:, :NT], in1=vg[:, :NT], op=ALU.mult),
                         [pub, b("vg")], [b("actT")])
            for half in range(2):
                wt, wb = wload(w_d, l, kb0 * 128, nkc, half * 512, 512)

                def evac_d(j, pt, pb, half=half):
                    P.op("dve", I("tensor_tensor", out=x_t[:TP, j, half * 512:(half + 1) * 512], in0=pt[:TP, :],
                                                          in1=x_t[:TP, j, half * 512:(half + 1) * 512], op=ALU.add), [pb, bxs[j]], [bxs[j]])
                tm_proj(wt, wb, nkc, lambda kc, j: actT[:, kc, j * TP:(j + 1) * TP], [b("actT")], 512, NS, TP, evac_d)

    def final_norm(t, sample):
        NS, TP = (1, 64) if sample else (4, 128)
        ld("sp", gvr[:], lnf_rep[:, 0:512], "gvr")
        ld("sp", bsr[:], lnf_rep[:, 512:1024], "bsr")
        lnh = [(gvr, b("gvr")), (bsr, b("bsr"))]
        for j in range(NS):
            bx = b(f"x{j}")
            P.op("act", I("activation", out=sqs[:TP, :], in_=x_t[:TP, j, :], func=AF.Square), [bx], [b("sqs")])
            P.op("dve", I("tensor_reduce", out=small[:TP, j:j + 1], in_=sqs[:TP, :], axis=AX.X, op=ALU.add), [b("sqs")], [b("small")])
            pow_rstd(small[:TP, j:j + 1], D, TP)
            for half in range(2):
                st, sbf = stage()
                P.op("dve", I("scalar_tensor_tensor",
                    out=st[:TP, :], in0=x_t[:TP, j, half * 512:(half + 1) * 512], scalar=small[:TP, j:j + 1],
                    in1=lnh[half][0][:TP, :], op0=ALU.mult, op1=ALU.mult), [bx, b("small"), lnh[half][1]], [sbf])
                if sample:
                    store(y_s[:, half * 512:(half + 1) * 512], st[:TP, :], sbf, "y_s")
                else:
                    store(y_p[t * TILE + j * 128:t * TILE + (j + 1) * 128, half * 512:(half + 1) * 512], st[:TP, :], sbf, "y_p")

    def mem_prologue():
        for kb in range(2):
            P.dma("pool", I("dma_start", out=mk_tok[:, kb, :], in_=memp[kb * 128:(kb + 1) * 128, :]),
                  b("dram_in"), b("mk_tok"), b("mk_tok"))
        for kb in range(2):
            for half in range(2):
                tp_ap, tb = trbank()
                for cc in range(4):
                    c = half * 4 + cc
                    P.op("pe", I("transpose", out=tp_ap[:, cc * 128:(cc + 1) * 128],
                                                                                    in_=mk_tok[:, kb, c * 128:(c + 1) * 128], identity=ident[:]),
                         [b("mk_tok"), b("ident")], [tb])
                P.op("dve", I("tensor_copy",
                    out=memT[:, half * 4:half * 4 + 4, kb * 128:(kb + 1) * 128], in_=tp_ap.rearrange("p (c k) -> p c k", c=4)), [tb], [b("memT")])
        for l in range(depth):
            for (wsrc, outd, is_k) in ((w_mk, mk_p, True), (w_mv, mv_p, False)):
                for half in range(2):
                    wt, wb = wload(wsrc, l, 0, 8, half * 512, 512)
                    for kb in range(2):
                        pt, pb = mmbank()
                        for kc in range(8):
                            P.op("pe", I("matmul", pt[:, :], lhsT=memT[:, kc, kb * 128:(kb + 1) * 128], rhs=wt[:, kc, :],
                                                                                     start=(kc == 0), stop=(kc == 7)), [wb, b("memT")], [pb])
                        st, sbf = stage()
                        P.op("act", I("activation", out=st[:, :], in_=pt[:, :], func=AF.Identity), [pb], [sbf])
                        store(outd[l, kb * 128:(kb + 1) * 128, half * 512:(half + 1) * 512], st[:, :], sbf, "mkv_out")
                        if not is_k:
                            P.op("dve", I("tensor_copy", out=mvb[:, kb, half * 512:(half + 1) * 512], in_=st[:, :]),
                                 [sbf], [b("mvb")])
                    if is_k:
                        for c in range(4):
                            fc = half * 4 + c
                            pt, pb = mmbank()
                            for kc in range(8):
                                P.op("pe", I("matmul", pt[:, :NMEM], lhsT=wt[:, kc, c * 128:(c + 1) * 128], rhs=memT[:, kc, :],
                                                                                       start=(kc == 0), stop=(kc == 7)), [wb, b("memT")], [pb])
                            P.op("dve", I("tensor_copy", out=mkT[:, fc, :], in_=pt[:, :NMEM]), [pb], [b("mkT")])
            P.dma("sp", I("dma_start", out=mkT_d[l], in_=mkT[:]), b("mkT"), b(f"mkT_d{l}"), b("mkT"))
            P.dma("sp", I("dma_start", out=mvb_d[l], in_=mvb[:]), b("mvb"), b(f"mvb_d{l}"), b("mvb"))

    cast_layer(0)
    mem_prologue()
    for t in range(n_tiles):
        for j in range(4):
            P.dma("sp", I("dma_start", out=x_t[:, j, :], in_=xp[t * TILE + j * 128:t * TILE + (j + 1) * 128, :]),
                  b("dram_in"), b(f"x{j}"), b(f"x{j}"))
        for l in range(depth):
            if t == 0 and l + 1 < depth:
                cast_layer(l + 1)
            layer(l, t, False)
        final_norm(t, False)
    if with_sample:
        P.dma("sp", I("dma_start", out=x_t[:64, 0, :], in_=xs[:, :]), b("dram_in"), b("x0"), b("x0"))
        for l in range(depth):
            layer(l, 0, True)
        final_norm(0, True)
    P.finish()

    sems = {}
    for k in P.semkeys:
        sems[k] = es.enter_context(nc.semaphore("s_" + "_".join(str(v) for v in k)))
    engmap = {"pe": "tensor", "act": "scalar", "dve": "vector", "pool": "gpsimd", "sp": "sync"}
    with es:
        with nc.Block() as block:
            def make(engname):
                def body(eng):
                    for (waits, fn, inc) in P.q[engname]:
                        for (k, v) in waits:
                            eng.wait_ge(sems[k], v)
                        if fn is not None:
                            ins = fn(eng)
                            ins.then_inc(sems[inc[0]], inc[1])
                return body
            for en in ENGS:
                getattr(block, engmap[en])(make(en))
    counts = {e: len(P.q[e]) for e in ENGS}
    return nc, counts


def _host_consts(inp):
    L = DEPTH
    f = np.float32

    def fm(v, nch):
        return np.ascontiguousarray(np.asarray(v, f).reshape(nch, 128).T)
    cols = []
    for name in ("ln_mix", "ln_mem", "ln_ffn"):
        for l in range(L):
            cols.append(fm(inp[name][l], 8))
    cols.append(fm(inp["ln_final"], 8))
    for name in ("g_out_sgu", "g_out_sb"):
        for l in range(L):
            cols.append(fm(inp[name][l], 4))
    gains_fm = np.ascontiguousarray(np.concatenate(cols, axis=1))
    gv = np.asarray(inp["g_sgu_v"], f).reshape(1, L * 512)
    gv_rep = np.ascontiguousarray(np.broadcast_to(gv, (128, L * 512)))
    bs = np.asarray(inp["b_sgu"], f)
    bs_rep = np.ascontiguousarray(np.broadcast_to(bs.reshape(1, L * 512), (128, L * 512)))
    bs_s = np.tile(bs[:, :, :16], (1, 1, 4))
    bs_rep_s = np.ascontiguousarray(np.broadcast_to(bs_s.reshape(1, L * 256), (128, L * 256)))
    ws = np.asarray(inp["w_sgu"], f)
    wsT = np.ascontiguousarray(ws.transpose(0, 1, 3, 2))
    wsT_s = np.zeros((L, 4, 64, 64), f)
    for s in range(4):
        wsT_s[:, :, s * 16:(s + 1) * 16, s * 16:(s + 1) * 16] = wsT[:, :, :16, :16]
    k = np.arange(128)
    tri = (k[:, None] < k[None, :]).astype(f)
    negtri = -(k[:, None] >= k[None, :]).astype(f)
    k64 = np.arange(64)
    same = (k64[:, None] // 16) == (k64[None, :] // 16)
    bd = (same & (k64[:, None] <= k64[None, :])).astype(f)
    newm = (same & (k64[:, None] < k64[None, :])).astype(f)
    newmask = np.zeros((128, 512), f)
    newmask[:64, :] = np.tile(newm, (1, 8))
    m = np.zeros((128, 960), f)
    m[:, 832:960] = np.eye(128, dtype=f)
    m[:, 0:128] = tri
    m[:, 128:256] = negtri
    m[:64, 256:320] = bd
    m[:, 320:832] = newmask
    lnf_rep = np.ascontiguousarray(np.broadcast_to(np.asarray(inp["ln_final"], f).reshape(1, D), (128, D)))
    return dict(gains_fm=gains_fm, gv_rep=gv_rep, bs_rep=bs_rep, bs_rep_s=bs_rep_s, wsT=wsT, wsT_s=wsT_s, masks=m, lnf_rep=lnf_rep)


_CACHE = {}


def kernel(**inputs):
    inp = {k: np.asarray(v) for k, v in inputs.items()}
    n_tiles = int(inp.pop("_n_tiles", SEQ // TILE)) if "_n_tiles" in inp else SEQ // TILE
    depth = int(inp.pop("_depth", DEPTH)) if "_depth" in inp else DEPTH
    key = (n_tiles, depth)
    if key not in _CACHE:
        _CACHE[key] = build(n_tiles, depth, os.environ.get("KNOSAMPLE") is None)
    nc, counts = _CACHE[key]
    consts = _host_consts(inp)
    f = np.float32
    shared = {
        "w_in": inp["w_in"], "w_out": inp["w_out"], "w_mq": inp["w_mq"], "w_mk": inp["w_mk"], "w_mv": inp["w_mv"],
        "w_mo": inp["w_mo"], "w_g": inp["w_ffn_gate"], "w_u": inp["w_ffn_up"], "w_d": inp["w_ffn_down"],
    }
    shared = {k: np.ascontiguousarray(v, dtype=f) for k, v in shared.items()}
    shared.update(consts)
    in_maps = []
    for c in range(8):
        bq = c // 2
        s0 = c * NSTREAM
        m = dict(shared)
        m["xp"] = np.ascontiguousarray(inp["x_prompt"][bq], dtype=f)
        m["xs"] = np.ascontiguousarray(inp["x_sample"][s0:s0 + NSTREAM].reshape(64, D), dtype=f)
        m["csk"] = np.ascontiguousarray(inp["cache_sb_k"][:, s0:s0 + NSTREAM].reshape(DEPTH, NSTREAM, PAST, 512), dtype=f)
        m["csv"] = np.ascontiguousarray(inp["cache_sb_v"][:, s0:s0 + NSTREAM].reshape(DEPTH, NSTREAM, PAST, 512), dtype=f)
        m["cmk"] = np.ascontiguousarray(inp["cache_mem_k"][:, s0:s0 + NSTREAM].reshape(DEPTH, NSTREAM, NMEM, D), dtype=f)
        m["cmv"] = np.ascontiguousarray(inp["cache_mem_v"][:, s0:s0 + NSTREAM].reshape(DEPTH, NSTREAM, NMEM, D), dtype=f)
        m["memp"] = np.ascontiguousarray(inp["mem_prompt"][bq], dtype=f)
        in_maps.append(m)
    ncores = int(os.environ.get("KCORES", "8"))
    res = run_bass_kernel_spmd(nc, in_maps[:ncores], core_ids=list(range(ncores)))
    R = list(res.results) + [res.results[0]] * (8 - ncores)
    B4 = 4

    def g(c, name):
        return np.asarray(R[c][name], dtype=f)
    y_prompt = np.stack([g(2 * bq, "y_p") for bq in range(B4)])
    y_sample = np.concatenate([g(c, "y_s").reshape(NSTREAM, DSEQ, D) for c in range(8)], axis=0)
    sb_k_prompt = np.stack([g(2 * bq, "sbk_p") for bq in range(B4)], axis=1).reshape(DEPTH, B4, SEQ, 8, 64)
    sb_v_prompt = np.stack([g(2 * bq, "sbv_p") for bq in range(B4)], axis=1).reshape(DEPTH, B4, SEQ, 8, 64)
    mem_k_prompt = np.stack([g(2 * bq, "mk_p") for bq in range(B4)], axis=1).reshape(DEPTH, B4, NMEM, 4, 256)
    mem_v_prompt = np.stack([g(2 * bq, "mv_p") for bq in range(B4)], axis=1).reshape(DEPTH, B4, NMEM, 4, 256)
    sb_k_sample = np.concatenate([g(c, "sbk_s").reshape(DEPTH, NSTREAM, DSEQ, 8, 64) for c in range(8)], axis=1)
    sb_v_sample = np.concatenate([g(c, "sbv_s").reshape(DEPTH, NSTREAM, DSEQ, 8, 64) for c in range(8)], axis=1)
    sgu_v_sample = np.concatenate([g(c, "sguv_s").reshape(DEPTH, NSTREAM, DSEQ, 4, 128) for c in range(8)], axis=1)
    return (y_prompt, y_sample, sb_k_prompt, sb_v_prompt, mem_k_prompt, mem_v_prompt, sb_k_sample, sb_v_sample, sgu_v_sample)
```
